# Optimizing a Trainium2 kernel written in Bass

```python
import jax
import jax.numpy as jnp
from jax import lax
import numpy as np


D_MODEL = 1024
BATCH = 2
SEQ = 8192
DEPTH = 2
DEC_BATCH = 16
DEC_SEQ = 4096
PAST_LEN = 128

HEAD_DIM = 64
A_Q_HEADS = 6
A_KV_HEADS = 2
A_GROUP = A_Q_HEADS // A_KV_HEADS
AXIAL_THETA = 10000.0
Q_BLOCK = 128
GRID_W = 64
B_HEADS = 6
DIL_CONFIGS = ((128, 1), (512, 4), (2048, 16))
B_HEADS_PER_CFG = B_HEADS // len(DIL_CONFIGS)
ROPE_THETA = 500000.0
ROT_DIMS = HEAD_DIM // 4
C_HEADS = 6
CONV_K = 5
CHUNK = 64
D_HEADS = 6
N_KEYS = 128
N_EXPERTS = N_KEYS * N_KEYS
PEER_HEADS = 8
PEER_QDIM = 256
PEER_HALF = PEER_QDIM // 2
PEER_TOPK = 16
PEER_BLOCK = 128
EPS = 1e-6

A_WIDTH = A_Q_HEADS * HEAD_DIM
A_KV_WIDTH = A_KV_HEADS * HEAD_DIM
B_WIDTH = B_HEADS * HEAD_DIM
C_WIDTH = C_HEADS * HEAD_DIM
D_WIDTH = D_HEADS * HEAD_DIM
IN_SPLITS = (A_WIDTH, A_KV_WIDTH, A_KV_WIDTH, B_WIDTH, B_WIDTH, B_WIDTH, 3 * C_WIDTH, 2 * C_HEADS, 2 * C_HEADS, C_WIDTH, D_WIDTH, D_WIDTH, 2 * D_WIDTH, D_WIDTH)
IN_WIDTH = sum(IN_SPLITS)
OUT_WIDTH = A_WIDTH + B_HEADS_PER_CFG * HEAD_DIM + C_WIDTH + D_WIDTH

kernel_name = 'hybrid_parallel_heads_peer_encoder'


def _rmsnorm(x, w):
    xf = x.astype(jnp.float32)
    y = xf * lax.rsqrt(jnp.mean(xf * xf, axis=-1, keepdims=True) + EPS)
    return (y * w.astype(jnp.float32)).astype(x.dtype)


def _l2norm(x):
    return x * lax.rsqrt(jnp.sum(x * x, axis=-1, keepdims=True) + EPS)


def _rope(x, pos, theta, rot):
    half = rot // 2
    inv_freq = theta ** (-jnp.arange(half, dtype=jnp.float32) / half)
    ang = pos.astype(jnp.float32)[:, None] * inv_freq[None, :]
    cos = jnp.cos(ang)[:, None, :]
    sin = jnp.sin(ang)[:, None, :]
    xf = x.astype(jnp.float32)
    x1 = xf[..., :half]
    x2 = xf[..., half:rot]
    out = jnp.concatenate([x1 * cos - x2 * sin, x2 * cos + x1 * sin, xf[..., rot:]], axis=-1)
    return out.astype(x.dtype)


def _axial_rope(x, row, col):
    half = HEAD_DIM // 2
    return jnp.concatenate([_rope(x[..., :half], row, AXIAL_THETA, half), _rope(x[..., half:], col, AXIAL_THETA, half)], axis=-1)


def _split_cols(z):
    parts = []
    off = 0
    for width in IN_SPLITS:
        parts.append(z[..., off:off + width])
        off += width
    return parts


def _flip(t):
    return jnp.flip(t, axis=1)


def _to_chunks(x):
    B, S, H = x.shape[:3]
    x = x.reshape(B, S // CHUNK, CHUNK, H, *x.shape[3:])
    return jnp.moveaxis(jnp.moveaxis(x, 1, 0), 3, 2)


def _from_chunks(y):
    n, B, H, C, d = y.shape
    return y.transpose(1, 0, 3, 2, 4).reshape(B, n * C, H, d)


def _axial_gqa(q, k, v, qn_w, kn_w, row, col):
    B, S = q.shape[:2]
    q = _axial_rope(_rmsnorm(q, qn_w), row, col)
    k = _axial_rope(_rmsnorm(k, kn_w), row, col)
    scale = HEAD_DIM ** -0.5
    qb = q.reshape(B, S // Q_BLOCK, Q_BLOCK, A_KV_HEADS, A_GROUP, HEAD_DIM).transpose(1, 0, 2, 3, 4, 5)

    def block(q_blk):
        s = jnp.einsum('bqkgd,bskd->bkgqs', q_blk, k).astype(jnp.float32) * scale
        p = jax.nn.softmax(s, axis=-1).astype(v.dtype)
        return jnp.einsum('bkgqs,bskd->bqkgd', p, v)

    o = lax.map(block, qb)
    return o.transpose(1, 0, 2, 3, 4, 5).reshape(B, S, A_WIDTH)


def _dilated_group(q, k, v, window, dil):
    B, S, H, D = q.shape
    W = window // (2 * dil)
    L = S // dil
    nb = -(-L // W)
    Lp = nb * W

    def to_sub(x):
        x = x.reshape(B, L, dil, H, D).transpose(0, 2, 1, 3, 4)
        return jnp.pad(x, ((0, 0), (0, 0), (0, Lp - L), (0, 0), (0, 0)))

    def banded(x):
        xp = jnp.pad(x, ((0, 0), (0, 0), (W, W), (0, 0), (0, 0)))
        return jnp.concatenate([xp[:, :, i * W:i * W + Lp].reshape(B, dil, nb, W, H, D) for i in range(3)], axis=3)

    qs = to_sub(q).reshape(B, dil, nb, W, H, D)
    ks = banded(to_sub(k))
    vs = banded(to_sub(v))
    rel = (jnp.arange(3 * W)[None, :] - W) - jnp.arange(W)[:, None]
    kpos = jnp.arange(nb)[:, None] * W + jnp.arange(3 * W)[None, :] - W
    mask = (jnp.abs(rel) <= W)[None, :, :] & ((kpos >= 0) & (kpos < L))[:, None, :]
    s = jnp.einsum('brnqhd,brnkhd->brnhqk', qs, ks).astype(jnp.float32) * (HEAD_DIM ** -0.5)
    s = jnp.where(mask[None, None, :, None], s, -jnp.inf)
    lse = jax.nn.logsumexp(s, axis=-1, keepdims=True)
    p = jnp.exp(s - lse).astype(v.dtype)
    o = jnp.einsum('brnhqk,brnkhd->brnqhd', p, vs)

    def from_sub(y):
        y = y.reshape(B, dil, Lp, *y.shape[4:])[:, :, :L]
        y = jnp.swapaxes(y, 1, 2)
        return y.reshape(B, S, *y.shape[3:])

    o = from_sub(o)
    lse = from_sub(jnp.swapaxes(lse[..., 0], 3, 4))
    return o, lse


def _dilated_mixture(q, k, v, pos):
    B, S = q.shape[:2]
    q = _rope(q, pos, ROPE_THETA, ROT_DIMS)
    k = _rope(k, pos, ROPE_THETA, ROT_DIMS)
    outs, lses = [], []
    for g, (window, dil) in enumerate(DIL_CONFIGS):
        hs = slice(g * B_HEADS_PER_CFG, (g + 1) * B_HEADS_PER_CFG)
        o, lse = _dilated_group(q[:, :, hs], k[:, :, hs], v[:, :, hs], window, dil)
        outs.append(o.astype(jnp.float32))
        lses.append(lse)
    wts = jax.nn.softmax(jnp.stack(lses, axis=0), axis=0)
    o = jnp.sum(wts[..., None] * jnp.stack(outs, axis=0), axis=0)
    return o.reshape(B, S, B_HEADS_PER_CFG * HEAD_DIM).astype(q.dtype)


def _gated_delta_chunked(q, k, v, beta, g):
    q, k, v, beta, g = [_to_chunks(t.astype(jnp.float32)) for t in (q, k, v, beta, g)]
    n, B, H, C, dk = q.shape
    dv = v.shape[-1]
    tri = jnp.tril(jnp.ones((C, C), dtype=bool))
    strict = jnp.tril(jnp.ones((C, C), dtype=bool), -1)
    gc = jnp.cumsum(g, axis=-1)
    decay = jnp.exp(jnp.where(tri, gc[..., :, None] - gc[..., None, :], -jnp.inf))
    kb = k * beta[..., None]
    m = jnp.where(strict, jnp.einsum('nbhid,nbhjd->nbhij', kb, k) * decay, 0.0)
    a = m + jnp.eye(C, dtype=jnp.float32)
    rhs = jnp.concatenate([v * beta[..., None], kb * jnp.exp(gc)[..., None]], axis=-1)
    sol = lax.linalg.triangular_solve(a, rhs, left_side=True, lower=True, unit_diagonal=True)
    u, w = sol[..., :dv], sol[..., dv:]
    qk = jnp.einsum('nbhid,nbhjd->nbhij', q, k) * decay

    def step(state, xs):
        qc, kc, uc, wc, gcc, qkc = xs
        v_new = uc - jnp.einsum('bhcd,bhde->bhce', wc, state)
        o = jnp.einsum('bhcd,bhde->bhce', qc * jnp.exp(gcc)[..., None], state) + jnp.einsum('bhij,bhje->bhie', qkc, v_new)
        glast = gcc[..., -1:]
        state = state * jnp.exp(glast)[..., None] + jnp.einsum('bhcd,bhce->bhde', kc * jnp.exp(glast - gcc)[..., None], v_new)
        return state, o

    state0 = jnp.zeros((B, H, dk, dv), jnp.float32)
    _, o = lax.scan(step, state0, (q, k, u, w, gc, qk))
    return _from_chunks(o)


def _gdn_mixer(qkv, beta_logit, a_logit, gate, conv_w, a_log, dt_bias, norm_w):
    B, S, ch = qkv.shape
    w = conv_w.reshape(CONV_K, 1, ch)
    qkv = jax.nn.silu(lax.conv_general_dilated(qkv, w, window_strides=(1,), padding=[(CONV_K // 2, CONV_K // 2)], dimension_numbers=('NWC', 'WIO', 'NWC'), feature_group_count=ch))
    qkv = qkv.astype(jnp.float32)
    q = _l2norm(qkv[..., :C_WIDTH].reshape(B, S, C_HEADS, HEAD_DIM)) * (HEAD_DIM ** -0.5)
    k = _l2norm(qkv[..., C_WIDTH:2 * C_WIDTH].reshape(B, S, C_HEADS, HEAD_DIM))
    v = qkv[..., 2 * C_WIDTH:].reshape(B, S, C_HEADS, HEAD_DIM)
    beta = jax.nn.sigmoid(beta_logit.astype(jnp.float32)).reshape(B, S, 2, C_HEADS)
    g = -jnp.exp(a_log.astype(jnp.float32)) * jax.nn.softplus(a_logit.astype(jnp.float32).reshape(B, S, 2, C_HEADS) + dt_bias.astype(jnp.float32))
    o_f = _gated_delta_chunked(q, k, v, beta[:, :, 0], g[:, :, 0])
    o_b = _flip(_gated_delta_chunked(_flip(q), _flip(k), _flip(v), _flip(beta[:, :, 1]), _flip(g[:, :, 1])))
    o = _rmsnorm(o_f + o_b, norm_w) * jax.nn.silu(gate.astype(jnp.float32).reshape(B, S, C_HEADS, HEAD_DIM))
    return o.reshape(B, S, C_WIDTH).astype(gate.dtype)


def _hgrn2_chunked(q, k, v, logf):
    q, k, v, logf = [_to_chunks(t.astype(jnp.float32)) for t in (q, k, v, logf)]
    n, B, H, C, dk = q.shape
    dv = v.shape[-1]
    tri = jnp.tril(jnp.ones((C, C), dtype=bool))[:, :, None]
    b = jnp.cumsum(logf, axis=-2)

    def step(state, xs):
        qc, kc, vc, bc = xs
        dec = jnp.exp(jnp.where(tri, bc[:, :, :, None, :] - bc[:, :, None, :, :], -jnp.inf))
        att = jnp.einsum('bhic,bhijc->bhij', qc, dec * kc[:, :, None, :, :])
        o = jnp.einsum('bhic,bhce->bhie', qc * jnp.exp(bc), state) + jnp.einsum('bhij,bhje->bhie', att, vc)
        blast = bc[:, :, -1:, :]
        state = state * jnp.swapaxes(jnp.exp(blast), -1, -2) + jnp.einsum('bhjc,bhje->bhce', kc * jnp.exp(blast - bc), vc)
        return state, o

    state0 = jnp.zeros((B, H, dk, dv), jnp.float32)
    _, o = lax.scan(step, state0, (q, k, v, b))
    return _from_chunks(o)


def _hgrn2_mixer(q, i, f_logit, gate, lb_param, layer, norm_w):
    B, S, _ = q.shape
    lb_cum = jnp.cumsum(jax.nn.softmax(lb_param.astype(jnp.float32), axis=0), axis=0)
    lb = (lb_cum[layer] - lb_cum[0]).reshape(2, D_HEADS, HEAD_DIM)
    fl = f_logit.astype(jnp.float32).reshape(B, S, 2, D_HEADS, HEAD_DIM)
    logf = jnp.logaddexp(jnp.log(lb), jnp.log1p(-lb) + jax.nn.log_sigmoid(fl))
    k = (1.0 - lb) * jax.nn.sigmoid(-fl)
    qh = q.astype(jnp.float32).reshape(B, S, D_HEADS, HEAD_DIM)
    vh = i.astype(jnp.float32).reshape(B, S, D_HEADS, HEAD_DIM)
    o_f = _hgrn2_chunked(qh, k[:, :, 0], vh, logf[:, :, 0])
    o_b = _flip(_hgrn2_chunked(_flip(qh), _flip(k[:, :, 1]), _flip(vh), _flip(logf[:, :, 1])))
    o = _rmsnorm(o_f + o_b, norm_w) * jax.nn.silu(gate.astype(jnp.float32).reshape(B, S, D_HEADS, HEAD_DIM))
    return o.reshape(B, S, D_WIDTH).astype(gate.dtype)


def _peer(h, w_query, sub_keys, u_tab, v_tab):
    B, S, D = h.shape
    xb = h.reshape(B * S // PEER_BLOCK, PEER_BLOCK, D)

    def block(xt):
        q = (xt @ w_query).reshape(PEER_BLOCK, PEER_HEADS, 2, PEER_HALF)
        s = jnp.einsum('thpc,hpnc->thpn', q, sub_keys).astype(jnp.float32)
        sv, si = lax.top_k(s, PEER_TOPK)
        cand = (sv[:, :, 0, :, None] + sv[:, :, 1, None, :]).reshape(PEER_BLOCK, PEER_HEADS, PEER_TOPK * PEER_TOPK)
        cidx = (si[:, :, 0, :, None] * N_KEYS + si[:, :, 1, None, :]).reshape(PEER_BLOCK, PEER_HEADS, PEER_TOPK * PEER_TOPK)
        tv, ti = lax.top_k(cand, PEER_TOPK)
        eidx = jnp.take_along_axis(cidx, ti, axis=-1)
        g = jax.nn.softmax(tv, axis=-1)
        u = u_tab[eidx]
        act = jax.nn.gelu(jnp.einsum('td,thkd->thk', xt, u).astype(jnp.float32), approximate=False)
        return jnp.einsum('thk,thkd->td', (g * act).astype(v_tab.dtype), v_tab[eidx])

    return lax.map(block, xb).reshape(B, S, D).astype(h.dtype)


def _layer_mixers(h, l, w_in, a_qnorm_w, a_knorm_w, c_conv_w, c_a_log, c_dt_bias, c_norm_w, d_lb, d_norm_w, w_out, pos, row, col):
    B, S, _ = h.shape
    (a_q, a_k, a_v, b_q, b_k, b_v, c_qkv, c_beta, c_a, c_gate, d_q, d_i, d_f, d_gate) = _split_cols(h @ w_in[l])
    heads = lambda t: t.reshape(B, S, -1, HEAD_DIM)
    o_a = _axial_gqa(heads(a_q), heads(a_k), heads(a_v), a_qnorm_w[l], a_knorm_w[l], row, col)
    o_b = _dilated_mixture(heads(b_q), heads(b_k), heads(b_v), pos)
    o_c = _gdn_mixer(c_qkv, c_beta, c_a, c_gate, c_conv_w[l], c_a_log[l], c_dt_bias[l], c_norm_w[l])
    o_d = _hgrn2_mixer(d_q, d_i, d_f, d_gate, d_lb, l, d_norm_w[l])
    mix = jnp.concatenate([o_a, o_b, o_c, o_d], axis=-1)
    return mix @ w_out[l]


def _trunk(x, norm1_w, w_in, a_qnorm_w, a_knorm_w, c_conv_w, c_a_log, c_dt_bias, c_norm_w, d_lb, d_norm_w, w_out, norm2_w, peer_w_query, peer_sub_keys, peer_u, peer_v, final_norm_w):
    B, S, _ = x.shape
    rows = S // GRID_W
    pos = jnp.arange(S)
    row = jnp.repeat(jnp.arange(rows), GRID_W)
    col = jnp.tile(jnp.arange(GRID_W), rows)
    for l in range(DEPTH):
        h = _rmsnorm(x, norm1_w[l])
        x = x + _layer_mixers(h, l, w_in, a_qnorm_w, a_knorm_w, c_conv_w, c_a_log, c_dt_bias, c_norm_w, d_lb, d_norm_w, w_out, pos, row, col).astype(x.dtype)
        h = _rmsnorm(x, norm2_w[l])
        x = x + _peer(h, peer_w_query[l], peer_sub_keys[l], peer_u[l], peer_v[l])
    return _rmsnorm(x, final_norm_w)


def setup_inputs(seed: int = 0) -> dict:
    key = jax.random.key(seed)
    ks = jax.random.split(key, 19)

    def nrm(k, shape, scale):
        return jax.random.normal(k, shape, jnp.float32) * scale

    return {
        'x_prompt': nrm(ks[0], (BATCH, SEQ, D_MODEL), 1.0),
        'x_sample': nrm(ks[1], (DEC_BATCH, DEC_SEQ, D_MODEL), 1.0),
        'norm1_w': 1.0 + nrm(ks[2], (DEPTH, D_MODEL), 0.02),
        'w_in': nrm(ks[3], (DEPTH, D_MODEL, IN_WIDTH), D_MODEL ** -0.5),
        'a_qnorm_w': 1.0 + nrm(ks[4], (DEPTH, HEAD_DIM), 0.02),
        'a_knorm_w': 1.0 + nrm(ks[5], (DEPTH, HEAD_DIM), 0.02),
        'c_conv_w': nrm(ks[6], (DEPTH, CONV_K, 3 * C_WIDTH), CONV_K ** -0.5),
        'c_a_log': jnp.log(jax.random.uniform(ks[7], (DEPTH, 2, C_HEADS), jnp.float32, 1.0, 16.0)),
        'c_dt_bias': nrm(ks[8], (DEPTH, 2, C_HEADS), 0.5) - 3.0,
        'c_norm_w': 1.0 + nrm(ks[9], (DEPTH, HEAD_DIM), 0.02),
        'd_lb': nrm(ks[10], (DEPTH, 2, D_WIDTH), 1.0),
        'd_norm_w': 1.0 + nrm(ks[11], (DEPTH, HEAD_DIM), 0.02),
        'w_out': nrm(ks[12], (DEPTH, OUT_WIDTH, D_MODEL), OUT_WIDTH ** -0.5),
        'norm2_w': 1.0 + nrm(ks[13], (DEPTH, D_MODEL), 0.02),
        'peer_w_query': nrm(ks[14], (DEPTH, D_MODEL, PEER_HEADS * PEER_QDIM), D_MODEL ** -0.5),
        'peer_sub_keys': nrm(ks[15], (DEPTH, PEER_HEADS, 2, N_KEYS, PEER_HALF), PEER_HALF ** -0.5),
        'peer_u': nrm(ks[16], (DEPTH, N_EXPERTS, D_MODEL), D_MODEL ** -0.5),
        'peer_v': nrm(ks[17], (DEPTH, N_EXPERTS, D_MODEL), (PEER_HEADS * PEER_TOPK) ** -0.5),
        'final_norm_w': 1.0 + nrm(ks[18], (D_MODEL,), 0.02),
    }


def reference(x_prompt, x_sample, norm1_w, w_in, a_qnorm_w, a_knorm_w, c_conv_w, c_a_log, c_dt_bias, c_norm_w, d_lb, d_norm_w, w_out, norm2_w, peer_w_query, peer_sub_keys, peer_u, peer_v, final_norm_w):
    y_prompt = _trunk(x_prompt, norm1_w, w_in, a_qnorm_w, a_knorm_w, c_conv_w, c_a_log, c_dt_bias, c_norm_w, d_lb, d_norm_w, w_out, norm2_w, peer_w_query, peer_sub_keys, peer_u, peer_v, final_norm_w)
    y_sample = _trunk(x_sample, norm1_w, w_in, a_qnorm_w, a_knorm_w, c_conv_w, c_a_log, c_dt_bias, c_norm_w, d_lb, d_norm_w, w_out, norm2_w, peer_w_query, peer_sub_keys, peer_u, peer_v, final_norm_w)
    return (y_prompt, y_sample)
```

```python
import numpy as np
import concourse.bass as bass
import concourse.mybir as mybir
from concourse.bass_utils import run_bass_kernel_spmd

F32 = mybir.dt.float32
BF16 = mybir.dt.bfloat16
U32 = mybir.dt.uint32
I32 = mybir.dt.int32
AF = mybir.ActivationFunctionType
ALU = mybir.AluOpType
AX = mybir.AxisListType

D = 1024
INW = 5272
OUTW = 1280
EPS = 1e-6
NEG = -30000.0


class Sched:
    ENG = ("pe", "act", "dve", "pool", "sp")

    def __init__(self, nc, es):
        self.nc = nc
        self.h = {"pe": nc.tensor, "act": nc.scalar, "dve": nc.vector, "pool": nc.gpsimd, "sp": nc.sync}
        self.q = {e: [] for e in self.ENG}
        self.cnt = {e: 0 for e in self.ENG}
        self.sem = {e: es.enter_context(nc.semaphore("s_" + e)) for e in self.ENG}
        self.nslot = {"sp": 10, "act": 3, "pool": 10}
        self.slots = {}
        for qn, n in self.nslot.items():
            for i in range(n):
                nm = "d_%s%d" % (qn, i)
                self.sem[nm] = es.enter_context(nc.semaphore(nm))
                self.cnt[nm] = 0
                self.slots.setdefault(qn, []).append(nm)
        self.rr = {qn: 0 for qn in self.nslot}
        self.seen = {e: {} for e in self.ENG}
        self.lastw = {}
        self.readers = {}
        self.nins = 0

    def _deps(self, eng, r, w):
        deps = {}

        def add(src, c, same_ok):
            if src == eng and not same_ok:
                return
            if c > deps.get(src, 0):
                deps[src] = c

        raw_same = True
        for k in r:
            lw = self.lastw.get(k)
            if lw:
                add(lw[0], lw[1], raw_same)
        for k in w:
            lw = self.lastw.get(k)
            if lw:
                add(lw[0], lw[1], raw_same)
            for src, c in self.readers.get(k, {}).items():
                add(src, c, False)
        waits = []
        seen = self.seen[eng]
        for src, c in deps.items():
            if seen.get(src, 0) >= c:
                continue
            seen[src] = c
            waits.append((self.sem[src], c * (16 if src.startswith("d_") else 1)))
        return waits

    def _commit(self, src, r, w):
        c = self.cnt[src]
        for k in w:
            self.lastw[k] = (src, c)
            self.readers[k] = {}
        for k in r:
            self.readers.setdefault(k, {})[src] = c

    def op(self, eng, fn, r=(), w=()):
        waits = self._deps(eng, r, w)
        self.cnt[eng] += 1
        self.q[eng].append((waits, fn, (self.sem[eng], 1)))
        self._commit(eng, r, w)
        self.nins += 1

    def dma(self, qn, fn, r=(), w=()):
        slot = self.slots[qn][self.rr[qn] % self.nslot[qn]]
        self.rr[qn] += 1
        waits = self._deps(qn, r, w)
        seen = self.seen[qn]
        if seen.get(slot, 0) < self.cnt[slot]:
            seen[slot] = self.cnt[slot]
            waits.append((self.sem[slot], self.cnt[slot] * 16))
        self.cnt[slot] += 1
        self.q[qn].append((waits, fn, (self.sem[slot], 16)))
        self._commit(slot, r, w)
        self.nins += 1

    def barrier(self):
        for e in self.ENG:
            waits = []
            for src in self.cnt:
                if src == e:
                    continue
                c = self.cnt[src]
                if self.seen[e].get(src, 0) < c:
                    self.seen[e][src] = c
                    waits.append((self.sem[src], c * (16 if src.startswith("d_") else 1)))
            if waits:
                self.q[e].append((waits, None, None))
        self.lastw = {}
        self.readers = {}

    def emit(self, block):
        def run(e):
            def body(eng):
                for waits, fn, inc in self.q[e]:
                    if fn is None:
                        for s, v in waits:
                            eng.wait_ge(s, v)
                        continue
                    fuse = False
                    for s, v in (waits[:-1] if fuse else waits):
                        eng.wait_ge(s, v)
                    ins = fn(eng)
                    if fuse:
                        ins._wait_ge(waits[-1][0], waits[-1][1])
                    ins.then_inc(inc[0], inc[1])
            return body

        block.tensor(run("pe"))
        block.scalar(run("act"))
        block.vector(run("dve"))
        block.gpsimd(run("pool"))
        block.sync(run("sp"))


class Arena:
    def __init__(self, t, n):
        self.t = t
        self.n = n
        self.off = 0

    def reset(self):
        self.off = 0

    def alloc(self, *shape, parts=128, dt=None):
        n = 1
        for s in shape:
            n *= s
        if dt is BF16:
            nw = (n + 1) // 2
            assert self.off + nw <= self.n, ("arena overflow", self.off, nw, self.n)
            v = self.t[0:parts, self.off:self.off + nw].bitcast(BF16)[:, 0:n]
            self.off += nw
        else:
            assert self.off + n <= self.n, ("arena overflow", self.off, n, self.n)
            v = self.t[0:parts, self.off:self.off + n]
            self.off += n
        if len(shape) == 2:
            v = v.rearrange("p (a b) -> p a b", a=shape[0])
        elif len(shape) == 3:
            v = v.rearrange("p (a b c) -> p a b c", a=shape[0], b=shape[1])
        return v


DILS = (1, 4, 16)
_USED = {}
HAVE_C = True


def strip_geom(dil):
    VL = 512 + 64 * dil
    W = 2 * 64 * dil + 1152
    return VL, W


def host_consts(NS, SL, coupled):
    c = {}
    c["ident"] = np.eye(128, dtype=np.float32)
    T = NS * SL
    pos = np.zeros(T, np.int64)
    for s in range(NS):
        base = SL if (s == 1 and coupled) else 0
        pos[s * SL:(s + 1) * SL] = base + np.arange(SL)
    rope = np.zeros((T, 160), np.float32)
    row = (pos // 64).astype(np.float64)
    col = (pos % 64).astype(np.float64)
    invA = 10000.0 ** (-np.arange(16, dtype=np.float64) / 16)
    for j, p in enumerate((row, col)):
        ang = (p[:, None].astype(np.float32) * invA[None, :].astype(np.float32)).astype(np.float32).astype(np.float64)
        cs, sn = np.cos(ang), np.sin(ang)
        rope[:, j * 32:j * 32 + 16] = cs
        rope[:, j * 32 + 16:j * 32 + 32] = cs
        rope[:, 64 + j * 32:64 + j * 32 + 16] = -sn
        rope[:, 64 + j * 32 + 16:64 + j * 32 + 32] = sn
    invB = 500000.0 ** (-np.arange(8, dtype=np.float64) / 8)
    ang = (pos[:, None].astype(np.float32) * invB[None, :].astype(np.float32)).astype(np.float32).astype(np.float64)
    cs, sn = np.cos(ang), np.sin(ang)
    rope[:, 128:136] = cs
    rope[:, 136:144] = cs
    rope[:, 144:152] = -sn
    rope[:, 152:160] = sn
    c["rope"] = rope
    strips = []
    for dil in DILS:
        VL, W = strip_geom(dil)
        k = np.arange(128)[:, None]
        v = np.arange(W)[None, :] - VL
        delta = k - v
        m = ((delta % dil) == 0) & (np.abs(delta) <= 64 * dil)
        strips.append(m.astype(np.float32))
    c["strips"] = np.concatenate(strips, axis=1)
    j = np.arange(128)[:, None]
    i = np.arange(128)[None, :]
    mats = [(j <= i).astype(np.float32) - (j <= 63), (j <= 63) + 0 * i, (j > 63) + 0 * i,
            (j >= i).astype(np.float32) - (j >= 64), (j >= 64) + 0 * i, (j < 64) + 0 * i,
            (j <= i), (j >= i)]
    c["dconst"] = np.concatenate([np.asarray(m, np.float32) for m in mats], axis=1)
    eye = np.eye(128, dtype=np.float32)
    m0 = (j <= i).astype(np.float32)
    m1 = (j >= i).astype(np.float32)
    c["cconst"] = np.concatenate([m0, m1, m0 - eye, m1 - eye], axis=1)
    pc = np.zeros((128, 32), np.float32)
    pc[:, 0:16] = np.arange(16)[None, :]
    pc[:, 16:31] = (16.0 * np.arange(1, 16))[None, :]
    c["pconst"] = pc
    cf = np.zeros((128, 2), np.float32)
    cf[:, 0] = 0.0 if coupled else NEG
    cf[:, 1] = 1.0 if coupled else 0.0
    c["cflag"] = cf
    return c


def build(NS, SL, DEPTH, dbg=(), stop_after=None, mode="all", lay=0, last=True):
    from contextlib import ExitStack
    T = NS * SL
    NT = T // 128
    groups = [(0, 2)] + [(s, 1) for s in range(2, NS)] if NS >= 2 else [(0, 1)]
    nc = bass.Bass("TRN2", target_bir_lowering=False)
    es = ExitStack()

    need_mix = mode in ("all", "mix")
    need_peer = mode in ("all", "peer")
    used_inputs = []

    def din(name, shape, dt=F32, need=True):
        if not need:
            shape = [1] * (len(shape) - 1) + [shape[-1] if len(shape) == 1 else 1]
            shape = [1] * len(shape)
            return None
        used_inputs.append(name)
        return nc.dram_tensor(name, list(shape), dt, kind="ExternalInput").ap()

    def dscr(name, shape, dt=F32):
        kind = "ExternalOutput" if name in dbg else "Internal"
        return nc.dram_tensor(name, list(shape), dt, kind=kind).ap()

    SW = sum(strip_geom(d)[1] for d in DILS)
    xin = din("xin", [T, D])
    rope_d = din("rope", [T, 160], need=need_mix)
    strips_d = din("strips", [128, SW], need=need_mix)
    cflag_d = din("cflag", [128, 2])
    ident_d = din("ident", [128, 128])
    norm1_w = din("norm1_w", [DEPTH, D], need=need_mix)
    w_in = din("w_in", [DEPTH, D, INW], need=need_mix)
    a_qnorm_w = din("a_qnorm_w", [DEPTH, 64], need=need_mix)
    a_knorm_w = din("a_knorm_w", [DEPTH, 64], need=need_mix)
    w_out = din("w_out", [DEPTH, OUTW, D], need=need_mix)
    dconst_d = din("dconst", [128, 1024], need=need_mix)
    cconst_d = din("cconst", [128, 512], need=need_mix)
    pconst_d = din("pconst", [128, 32], need=need_peer)
    norm2_w = din("norm2_w", [DEPTH, D], need=need_peer)
    peer_w_query = din("peer_w_query", [DEPTH, D, 2048], need=need_peer)
    peer_sub_keys = din("peer_sub_keys", [DEPTH, 8, 2, 128, 128], need=need_peer)
    peer_u = din("peer_u", [DEPTH, 16384, D], need=need_peer)
    peer_v = din("peer_v", [DEPTH, 16384, D], need=need_peer)
    c_conv_w = din("c_conv_w", [DEPTH, 5, 1152], need=need_mix)
    c_a_log = din("c_a_log", [DEPTH, 2, 6], need=need_mix)
    c_dt_bias = din("c_dt_bias", [DEPTH, 2, 6], need=need_mix)
    c_norm_w = din("c_norm_w", [DEPTH, 64], need=need_mix)
    zc_d = dscr("zc_d", [NS * (SL + 4), 1152])
    cq_d = dscr("cq_d", [T, 1152])
    cbg_d = dscr("cbg_d", [T, 24])
    ocf_d = dscr("ocf_d", [T, 384])
    d_lb = din("d_lb", [2 if mode != "all" else DEPTH, 2, 384], need=need_mix)
    d_norm_w = din("d_norm_w", [DEPTH, 64], need=need_mix)
    of_d = dscr("of_d", [T, 384])
    final_norm_w = din("final_norm_w", [1, D], need=(mode == "all" or (mode == "peer" and last)))
    yout = nc.dram_tensor("yout", [T, D], F32, kind="ExternalOutput").ap()
    z_d = dscr("z_d", [T, INW])
    qT_d = dscr("qT_d", [10 * 128, T], BF16)
    vaug_d = dscr("vaug_d", [T, 520], BF16)
    mixT_d = dscr("mixT_d", [OUTW, T], BF16)
    xa_d = dscr("xa_d", [T, D])
    xb_d = dscr("xb_d", [T, D])

    with es:
        AW = 47 * 1024
        a_t = es.enter_context(nc.sbuf_tensor("arena", [128, AW], F32))
        A = Arena(a_t, AW)
        ps = [es.enter_context(nc.psum_tensor("ps%d" % i, [128, 512], F32)) for i in range(8)]
        S = Sched(nc, es)
        block = es.enter_context(nc.Block())

        def a32(*shape, parts=128):
            return A.alloc(*shape, parts=parts)

        def a16(*shape, parts=128):
            return A.alloc(*shape, parts=parts, dt=BF16)

        ident = a32(128)
        S.dma("sp", lambda e: e.dma_start(out=ident, in_=ident_d[:, :]), w=["ident"])
        cfl = a32(2)
        S.dma("sp", lambda e: e.dma_start(out=cfl, in_=cflag_d[:, :]), w=["cfl"])
        ones1 = a32(64)
        S.op("pool", lambda e: e.memset(ones1, 1.0), w=["ones1"])
        base_off = A.off

        def stage_reset():
            S.barrier()
            A.off = base_off
            for k in ("ident", "cfl", "ones1"):
                S.lastw[k] = ("sp", 0)

        def load_cast(dst, src_rows, ncols, key, piece=1024):
            wst = [a32(piece) for _ in range(2)]
            i = 0
            nk = dst.shape[1]
            for k in range(nk):
                for c0 in range(0, ncols, piece):
                    cw = min(piece, ncols - c0)
                    st = wst[i % 2]
                    sk = "wst%d" % (i % 2)
                    DMA(st[:, 0:cw], src_rows(k)[:, c0:c0 + cw], [], [sk])
                    if i % 2 == 0:
                        S.op("act", lambda e, st=st, k=k, c0=c0, cw=cw: e.activation(out=dst[:, k, c0:c0 + cw], in_=st[:, 0:cw], func=AF.Copy), r=[sk], w=[key])
                    else:
                        S.op("pool", lambda e, st=st, k=k, c0=c0, cw=cw: e.tensor_copy(out=dst[:, k, c0:c0 + cw], in_=st[:, 0:cw]), r=[sk], w=[key])
                    i += 1

        def rmsnorm_tile(X, kx, H, kh, nw, knw, ss):
            sq = ss["sq"]
            S.op("pool", lambda e: e.tensor_tensor(out=sq, in0=X, in1=X, op=ALU.mult), r=[kx], w=["sq"])
            S.op("dve", lambda e: e.tensor_reduce(out=ss["v"][:, 0:1], in_=sq, axis=AX.X, op=ALU.add), r=["sq"], w=["ss0"])
            S.op("act", lambda e: e.activation(out=ss["v"][:, 1:2], in_=ss["v"][:, 0:1], func=AF.Sqrt, bias=EPS, scale=1.0 / D), r=["ss0"], w=["ss1"])
            S.op("dve", lambda e: e.reciprocal(out=ss["v"][:, 2:3], in_=ss["v"][:, 1:2]), r=["ss1"], w=["ss2"])
            S.op("dve", lambda e: e.scalar_tensor_tensor(out=H, in0=X, scalar=ss["v"][:, 2:3], in1=nw, op0=ALU.mult, op1=ALU.mult), r=[kx, "ss2", knw], w=[kh])

        def mk_ss():
            return {"sq": a32(D), "v": a32(4)}

        def MM(out, lhsT, rhs, r, w, start=True, stop=True):
            S.op("pe", lambda e: e.matmul(out, lhsT=lhsT, rhs=rhs, start=start, stop=stop), r=r, w=w)

        def TR(out, in_, r, w):
            S.op("pe", lambda e: e.transpose(out=out, in_=in_, identity=ident), r=list(r) + ["ident"], w=w)

        def TT(eng, out, in0, in1, op, r, w):
            S.op(eng, lambda e: e.tensor_tensor(out=out, in0=in0, in1=in1, op=op), r=r, w=w)

        def TS(eng, out, in0, s1, s2, op0, op1, r, w):
            if s2 is None:
                S.op(eng, lambda e: e.tensor_scalar(out=out, in0=in0, scalar1=s1, scalar2=None, op0=op0), r=r, w=w)
            else:
                S.op(eng, lambda e: e.tensor_scalar(out=out, in0=in0, scalar1=s1, scalar2=s2, op0=op0, op1=op1), r=r, w=w)

        def ACTF(out, in_, func, r, w, scale=1.0, bias=0.0):
            S.op("act", lambda e: e.activation(out=out, in_=in_, func=func, bias=bias, scale=scale), r=r, w=w)

        def STT(out, in0, scalar, in1, op0, op1, r, w, accum_out=None):
            if accum_out is None:
                S.op("dve", lambda e: e.scalar_tensor_tensor(out=out, in0=in0, scalar=scalar, in1=in1, op0=op0, op1=op1), r=r, w=w)
            else:
                S.op("dve", lambda e: e.scalar_tensor_tensor(out=out, in0=in0, scalar=scalar, in1=in1, op0=op0, op1=op1, accum_out=accum_out), r=r, w=w)

        def DMA(out, in_, r, w, q="sp"):
            S.dma(q, lambda e: e.dma_start(out=out, in_=in_), r=r, w=w)

        def CP(eng, out, in_, r, w):
            if eng == "act":
                ACTF(out, in_, AF.Copy, r, w)
            else:
                S.op(eng, lambda e: e.tensor_copy(out=out, in_=in_), r=r, w=w)

        def RECIP(out, in_, r, w):
            S.op("dve", lambda e: e.reciprocal(out=out, in_=in_), r=r, w=w)

        def RED(out, in_, r, w):
            S.op("dve", lambda e: e.tensor_reduce(out=out, in_=in_, axis=AX.X, op=ALU.add), r=r, w=w)

        def MEMSET(eng, out, val, w):
            S.op(eng, lambda e: e.memset(out, val), w=w)

        def slot_order(dr):
            out = []
            slots = range(NS) if dr == 0 else range(NS - 1, -1, -1)
            for s in slots:
                tl = list(range(s * SL // 128, (s + 1) * SL // 128))
                if dr == 1:
                    tl = tl[::-1]
                for i, t in enumerate(tl):
                    first = (i == 0)
                    coupled = first and NS >= 2 and ((dr == 0 and s == 1) or (dr == 1 and s == 0))
                    out.append((t, first, coupled))
            return out

        x_src = xin
        if mode == "mix":
            xa_d = yout
        if mode == "peer":
            xa_d = xin
        if mode == "peer" and not last:
            xb_d = yout
        for l in range(DEPTH):
          dl = lay if mode != "all" else l
          if need_mix:
            stage_reset()
            wsb = a16(8, INW)
            nw = a32(D)
            DMA(nw, norm1_w[l:l + 1, :].partition_broadcast(128), [], ["nw"])
            load_cast(wsb, lambda k: w_in[l, k * 128:(k + 1) * 128, :], INW, "wsb")
            xt = [a32(D) for _ in range(2)]
            hb = [a32(D) for _ in range(2)]
            hT = [a16(8, 128) for _ in range(2)]
            zt = [a32(INW) for _ in range(2)]
            ss = mk_ss()
            for t in range(NT):
                b = t % 2
                X, H, HT, Z = xt[b], hb[b], hT[b], zt[b]
                kx, kh, kht, kz = "xt%d" % b, "hb%d" % b, "hT%d" % b, "zt%d" % b
                DMA(X, x_src[t * 128:(t + 1) * 128, :], [], [kx])
                rmsnorm_tile(X, kx, H, kh, nw, "nw", ss)
                for k in range(8):
                    pb = ps[k % 2]
                    kp = "ps%d" % (k % 2)
                    S.op("pe", lambda e, pb=pb, H=H, k=k: e.transpose(out=pb[:, 0:128], in_=H[:, k * 128:(k + 1) * 128], identity=ident), r=[kh, "ident"], w=[kp])
                    if k % 2 == 0:
                        S.op("act", lambda e, pb=pb, HT=HT, k=k: e.activation(out=HT[:, k, :], in_=pb[:, 0:128], func=AF.Copy), r=[kp], w=[kht])
                    else:
                        S.op("dve", lambda e, pb=pb, HT=HT, k=k: e.tensor_copy(out=HT[:, k, :], in_=pb[:, 0:128]), r=[kp], w=[kht])
                ci = 0
                for c0 in range(0, INW, 512):
                    cw = min(512, INW - c0)
                    pb = ps[2 + ci % 4]
                    kp = "ps%d" % (2 + ci % 4)
                    for k in range(8):
                        S.op("pe", lambda e, pb=pb, HT=HT, k=k, c0=c0, cw=cw: e.matmul(pb[:, 0:cw], lhsT=HT[:, k, :], rhs=wsb[:, k, c0:c0 + cw], start=(k == 0), stop=(k == 7)), r=[kht, "wsb"], w=[kp])
                    if ci % 2 == 0:
                        S.op("act", lambda e, pb=pb, Z=Z, c0=c0, cw=cw: e.activation(out=Z[:, c0:c0 + cw], in_=pb[:, 0:cw], func=AF.Copy), r=[kp], w=[kz])
                    else:
                        S.op("dve", lambda e, pb=pb, Z=Z, c0=c0, cw=cw: e.tensor_copy(out=Z[:, c0:c0 + cw], in_=pb[:, 0:cw]), r=[kp], w=[kz])
                    ci += 1
                S.dma("sp", lambda e, Z=Z, t=t: e.dma_start(out=z_d[t * 128:(t + 1) * 128, :], in_=Z), r=[kz], w=[("z", t)])
            if stop_after == "s1":
                break

            stage_reset()
            nwA = a32(512)
            for hh in range(8):
                src = a_qnorm_w if hh < 6 else a_knorm_w
                DMA(nwA[:, hh * 64:(hh + 1) * 64], src[l:l + 1, :].partition_broadcast(128), [], ["nwA"])
            zts = [a32(1792) for _ in range(2)]
            rps = [a32(160) for _ in range(2)]
            sqa, qn, t1, t2, qrA = a32(512), a32(512), a32(512), a32(512), a32(512)
            qrB = a32(768)
            t1b, t2b = a32(192), a32(192)
            sv = a32(24)
            stg = [a16(10, 128) for _ in range(2)]
            vst = [a16(8, 65) for _ in range(2)]
            for b in range(2):
                S.op("pool", lambda e, b=b: e.memset(vst[b], 1.0), w=["vst%d" % b])
            qT_v = qT_d.rearrange("(b p) t -> p b t", p=128)
            for t in range(NT):
                b = t % 2
                Z, RP, SG, VS = zts[b], rps[b], stg[b], vst[b]
                kz, krp, ksg, kvs = "pz%d" % b, "rp%d" % b, "stg%d" % b, "vst%d" % b
                S.dma("sp", lambda e, Z=Z, t=t: e.dma_start(out=Z, in_=z_d[t * 128:(t + 1) * 128, 0:1792]), r=[("z", t)], w=[kz])
                S.dma("sp", lambda e, RP=RP, t=t: e.dma_start(out=RP, in_=rope_d[t * 128:(t + 1) * 128, :]), w=[krp])
                S.op("dve", lambda e, Z=Z: e.tensor_tensor(out=sqa, in0=Z[:, 0:512], in1=Z[:, 0:512], op=ALU.mult), r=[kz], w=["sqa"])
                S.op("dve", lambda e: e.tensor_reduce(out=sv[:, 0:8], in_=sqa.rearrange("p (h d) -> p h d", h=8), axis=AX.X, op=ALU.add), r=["sqa"], w=["sv0"])
                S.op("act", lambda e: e.activation(out=sv[:, 8:16], in_=sv[:, 0:8], func=AF.Sqrt, bias=EPS, scale=1.0 / 64), r=["sv0"], w=["sv1"])
                S.op("dve", lambda e: e.reciprocal(out=sv[:, 16:24], in_=sv[:, 8:16]), r=["sv1"], w=["sv2"])
                S.op("dve", lambda e, Z=Z: e.tensor_tensor(out=qn.rearrange("p (h d) -> p h d", h=8), in0=Z[:, 0:512].rearrange("p (h d) -> p h d", h=8),
                                                           in1=sv[:, 16:24].unsqueeze(2).to_broadcast([128, 8, 64]), op=ALU.mult), r=[kz, "sv2"], w=["qn"])
                S.op("pool", lambda e: e.tensor_tensor(out=qn, in0=qn, in1=nwA, op=ALU.mult), r=["qn", "nwA"], w=["qn"])
                S.op("dve", lambda e, RP=RP: e.tensor_tensor(out=t1.rearrange("p (h d) -> p h d", h=8), in0=qn.rearrange("p (h d) -> p h d", h=8),
                                                             in1=RP[:, 0:64].unsqueeze(1).to_broadcast([128, 8, 64]), op=ALU.mult), r=["qn", krp], w=["t1"])
                qn5 = qn.rearrange("p (h r x d) -> p h r x d", h=8, r=2, x=2)
                t25 = t2.rearrange("p (h r x d) -> p h r x d", h=8, r=2, x=2)
                for x in range(2):
                    S.op("pool", lambda e, RP=RP, x=x: e.tensor_tensor(
                        out=t25[:, :, :, x, :], in0=qn5[:, :, :, 1 - x, :],
                        in1=RP[:, 64:128].rearrange("p (r x d) -> p r x d", r=2, x=2)[:, :, x, :].unsqueeze(1).to_broadcast([128, 8, 2, 16]), op=ALU.mult),
                        r=["qn", krp], w=["t2"])
                S.op("dve", lambda e: e.tensor_tensor(out=qrA[:, 0:384].rearrange("p (g k d) -> p k g d", g=3, k=2), in0=t1[:, 0:384].rearrange("p (k g d) -> p k g d", k=2, g=3),
                                                      in1=t2[:, 0:384].rearrange("p (k g d) -> p k g d", k=2, g=3), op=ALU.add), r=["t1", "t2"], w=["qrA"])
                S.op("dve", lambda e: e.tensor_tensor(out=qrA[:, 384:512], in0=t1[:, 384:512], in1=t2[:, 384:512], op=ALU.add), r=["t1", "t2"], w=["qrA"])
                S.op("act", lambda e, Z=Z: e.activation(out=qrB, in_=Z[:, 640:1408], func=AF.Copy), r=[kz], w=["qrB"])
                zb = Z[:, 640:1408].rearrange("p (h d) -> p h d", h=12)
                S.op("dve", lambda e, RP=RP, zb=zb: e.tensor_tensor(out=t1b.rearrange("p (h d) -> p h d", h=12), in0=zb[:, :, 0:16],
                                                                   in1=RP[:, 128:144].unsqueeze(1).to_broadcast([128, 12, 16]), op=ALU.mult), r=[kz, krp], w=["t1b"])
                t2b3 = t2b.rearrange("p (h d) -> p h d", h=12)
                for x in range(2):
                    S.op("pool", lambda e, RP=RP, zb=zb, x=x: e.tensor_tensor(
                        out=t2b3[:, :, x * 8:(x + 1) * 8], in0=zb[:, :, (1 - x) * 8:(2 - x) * 8],
                        in1=RP[:, 144 + x * 8:152 + x * 8].unsqueeze(1).to_broadcast([128, 12, 8]), op=ALU.mult), r=[kz, krp], w=["t2b"])
                S.op("dve", lambda e: e.tensor_tensor(out=qrB.rearrange("p (h d) -> p h d", h=12)[:, :, 0:16], in0=t1b.rearrange("p (h d) -> p h d", h=12),
                                                      in1=t2b3, op=ALU.add), r=["t1b", "t2b", "qrB"], w=["qrB"])
                for bi in range(10):
                    if bi < 4:
                        src, ksrc = qrA[:, bi * 128:(bi + 1) * 128], "qrA"
                    elif bi < 7:
                        src, ksrc = qrB[:, (bi - 4) * 128:(bi - 3) * 128], "qrB"
                    else:
                        src, ksrc = qrB[:, 384 + (bi - 7) * 128:384 + (bi - 6) * 128], "qrB"
                    pb = ps[bi % 4]
                    kp = "ps%d" % (bi % 4)
                    S.op("pe", lambda e, pb=pb, src=src: e.transpose(out=pb[:, 0:128], in_=src, identity=ident), r=[ksrc, "ident"], w=[kp])
                    if bi % 2 == 0:
                        S.op("act", lambda e, pb=pb, SG=SG, bi=bi: e.activation(out=SG[:, bi, :], in_=pb[:, 0:128], func=AF.Copy), r=[kp], w=[ksg])
                    else:
                        S.op("dve", lambda e, pb=pb, SG=SG, bi=bi: e.tensor_copy(out=SG[:, bi, :], in_=pb[:, 0:128]), r=[kp], w=[ksg])
                S.dma("sp", lambda e, SG=SG, t=t: e.dma_start(out=qT_v[:, :, t * 128:(t + 1) * 128], in_=SG), r=[ksg], w=[("qT", t)])
                S.op("act", lambda e, Z=Z, VS=VS: e.activation(out=VS[:, 0:2, 0:64], in_=Z[:, 512:640].rearrange("p (h d) -> p h d", h=2), func=AF.Copy), r=[kz], w=[kvs])
                S.op("pool", lambda e, Z=Z, VS=VS: e.tensor_copy(out=VS[:, 2:8, 0:64], in_=Z[:, 1408:1792].rearrange("p (h d) -> p h d", h=6)), r=[kz], w=[kvs])
                S.dma("sp", lambda e, VS=VS, t=t: e.dma_start(out=vaug_d[t * 128:(t + 1) * 128, :], in_=VS.rearrange("p h d -> p (h d)")), r=[kvs], w=[("va", t)])
            if stop_after == "s2":
                break

            stage_reset()
            strips = a16(SW)
            s32 = a32(1024)
            i = 0
            for c0 in range(0, SW, 1024):
                cw = min(1024, SW - c0)
                S.dma("sp", lambda e, c0=c0, cw=cw: e.dma_start(out=s32[:, 0:cw], in_=strips_d[:, c0:c0 + cw]), w=["s32"])
                S.op("act", lambda e, c0=c0, cw=cw: e.activation(out=strips[:, c0:c0 + cw], in_=s32[:, 0:cw], func=AF.Copy), r=["s32"], w=["strips"])
            soff = []
            o = 0
            for dil in DILS:
                soff.append(o)
                o += strip_geom(dil)[1]
            pbuf = [a16(512) for _ in range(3)]
            rsum = a32(512)
            bcs = a32(512)
            oh = [a16(512) for _ in range(2)]
            qS = [a16(6, 512) for _ in range(2)]
            TgMax = max(ns for _, ns in groups) * SL
            kTs_all = a16(4, TgMax)
            vS_all = a16(TgMax // 128, 520)
            qc_glob = 0
            hcount = 0
            for (s0, ns) in groups:
                Tg = ns * SL
                g0 = s0 * SL
                nkt = Tg // 128
                kTs = kTs_all
                vS = vS_all
                for bi, blk in enumerate((3, 7, 8, 9)):
                    for c0 in range(0, Tg, 2048):
                        cw = min(2048, Tg - c0)
                        DMA(kTs[:, bi, c0:c0 + cw], qT_d[blk * 128:(blk + 1) * 128, g0 + c0:g0 + c0 + cw], [("qT", (g0 + c0) // 128 + i2) for i2 in range(cw // 128)], ["kTs"])
                for c0 in range(0, nkt, 8):
                    cn = min(8, nkt - c0)
                    DMA(vS[:, c0:c0 + cn, :], vaug_d[g0 + c0 * 128:g0 + (c0 + cn) * 128, :].rearrange("(k p) c -> p k c", p=128), [("va", g0 // 128 + c0 + i2) for i2 in range(cn)], ["vS"])
                for qc in range(Tg // 512):
                    q0 = qc * 512
                    sq = q0 // SL
                    QS = qS[qc_glob % 2]
                    kq = "qS%d" % (qc_glob % 2)
                    qc_glob += 1
                    for bi, blk in enumerate((0, 1, 2, 4, 5, 6)):
                        DMA(QS[:, bi, :], qT_d[blk * 128:(blk + 1) * 128, g0 + q0:g0 + q0 + 512], [("qT", (g0 + q0) // 128 + i2) for i2 in range(4)], [kq])
                    heads = [("A", h) for h in range(6)] + [("B", j) for j in range(2)]
                    for kind, hx in heads:
                        items = []
                        if kind == "A":
                            half = hx // 3
                            for kt in range(nkt):
                                items.append((0, hx % 3, half, kt, half, None))
                            f0 = hx * 64
                        else:
                            half = hx
                            for g, dil in enumerate(DILS):
                                lo = -((-(q0 - 64 * dil - 127)) // 128)
                                hi = (q0 + 511 + 64 * dil) // 128
                                for kt in range(max(lo, 0), min(hi, nkt - 1) + 1):
                                    v0 = q0 - 128 * kt
                                    items.append((1 + g, 3 + g, half, kt, 2 + 2 * g + hx, soff[g] + v0 + strip_geom(dil)[0]))
                            f0 = 384 + hx * 64
                        po = ps[4 + hcount % 2]
                        kpo = "ps%d" % (4 + hcount % 2)
                        OH = oh[hcount % 2]
                        koh = "oh%d" % (hcount % 2)
                        hcount += 1
                        hs = slice(half * 64, half * 64 + 64)
                        for idx, (kb, qb, hf, kt, vh, mo) in enumerate(items):
                            pi = idx % 3
                            pss = ps[pi]
                            kps = "ps%d" % pi
                            P = pbuf[pi]
                            kpb = "pbuf%d" % pi
                            S.op("pe", lambda e, pss=pss, kb=kb, kt=kt, qb=qb, QS=QS, hs=hs: e.matmul(pss[:, 0:512], lhsT=kTs[hs, kb, kt * 128:(kt + 1) * 128], rhs=QS[hs, qb, :], start=True, stop=True),
                                 r=["kTs", kq], w=[kps])
                            sk = (kt * 128) // SL
                            bias = 0.0 if sk == sq else cfl[:, 0:1]
                            S.op("act", lambda e, pss=pss, P=P, bias=bias: e.activation(out=P, in_=pss[:, 0:512], func=AF.Exp, bias=bias, scale=0.125), r=[kps, "cfl"], w=[kpb])
                            if mo is not None:
                                S.op("dve", lambda e, P=P, mo=mo: e.tensor_tensor(out=P, in0=P, in1=strips[:, mo:mo + 512], op=ALU.mult), r=[kpb, "strips"], w=[kpb])
                            S.op("pe", lambda e, po=po, vh=vh, kt=kt, P=P, idx=idx, n=len(items): e.matmul(po[0:65, 0:512], lhsT=vS[:, kt, vh * 65:(vh + 1) * 65], rhs=P, start=(idx == 0), stop=(idx == n - 1)),
                                 r=["vS", kpb], w=[kpo])
                        S.op("dve", lambda e, po=po: e.reciprocal(out=rsum[64:65, :], in_=po[64:65, 0:512]), r=[kpo], w=["rsum"])
                        S.op("pe", lambda e: e.matmul(ps[6][0:64, 0:512], lhsT=ones1[64:65, 0:64], rhs=rsum[64:65, :], start=True, stop=True), r=["rsum", "ones1"], w=["ps6"])
                        S.op("act", lambda e: e.activation(out=bcs[0:64, :], in_=ps[6][0:64, 0:512], func=AF.Copy), r=["ps6"], w=["bcs"])
                        S.op("dve", lambda e, po=po, OH=OH: e.tensor_tensor(out=OH[0:64, :], in0=po[0:64, 0:512], in1=bcs[0:64, :], op=ALU.mult), r=[kpo, "bcs"], w=[koh])
                        DMA(mixT_d[f0:f0 + 64, g0 + q0:g0 + q0 + 512], OH[0:64, :], [koh], [("mx", f0, g0 + q0)])
            if stop_after == "s3":
                break


            stage_reset()
            PADL = SL + 4
            zpad = a32(1152)
            MEMSET("pool", zpad, 0.0, ["zpad"])
            for s in range(NS):
                DMA(zc_d[s * PADL:s * PADL + 2, :], zpad[0:2, :], ["zpad"], [("zcp", s, 0)])
                DMA(zc_d[s * PADL + SL + 2:s * PADL + SL + 4, :], zpad[0:2, :], ["zpad"], [("zcp", s, 1)])
                for r0 in range(0, SL, 512):
                    DMA(zc_d[s * PADL + 2 + r0:s * PADL + 2 + r0 + 512, :], z_d[s * SL + r0:s * SL + r0 + 512, 1792:2944],
                        [("z", (s * SL + r0) // 128 + i2) for i2 in range(4)], [("zc", s, r0 // 128 + i2) for i2 in range(4)])
            if NS >= 2:
                nb = a32(2, 1152)
                DMA(nb[0:2, 0, :], z_d[SL - 2:SL, 1792:2944], [("z", SL // 128 - 1)], ["nb0"])
                DMA(nb[0:2, 1, :], z_d[SL:SL + 2, 1792:2944], [("z", SL // 128)], ["nb1"])
                TS("dve", nb[0:2], nb[0:2], cfl[0:2, 1:2], None, ALU.mult, None, ["nb0", "nb1", "cfl"], ["nb0", "nb1"])
                DMA(zc_d[PADL:PADL + 2, :], nb[0:2, 0, :], ["nb0"], [("zcp", 1, 0)])
                DMA(zc_d[SL + 2:SL + 4, :], nb[0:2, 1, :], ["nb1"], [("zcp", 0, 1)])
            cw_ = a32(5, 1152)
            for k in range(5):
                DMA(cw_[:, k, :], c_conv_w[l, k:k + 1, :].partition_broadcast(128), [], ["cw"])
            dtb, nea = a32(12), a32(12)
            DMA(dtb, c_dt_bias.rearrange("l a c -> l (a c)")[l:l + 1, :].partition_broadcast(128), [], ["dtb"])
            DMA(nea, c_a_log.rearrange("l a c -> l (a c)")[l:l + 1, :].partition_broadcast(128), [], ["nea"])
            ACTF(nea, nea, AF.Exp, ["nea"], ["nea"])
            TS("dve", nea, nea, -1.0, None, ALU.mult, None, ["nea"], ["nea"])
            xk = [[a32(1152) for _ in range(5)] for _ in range(2)]
            accA, accB, tmpc = a32(1152), a32(1152), a32(1152)
            qkv = [a32(1152) for _ in range(2)]
            zb_ = [a32(24) for _ in range(2)]
            bgs = [a32(24) for _ in range(2)]
            sqc = a32(768)
            sc = a32(36)
            for t in range(NT):
                b = t % 2
                s, i = divmod(t, SL // 128)
                base = s * PADL + i * 128
                deps = [("zc", s, i), ("zcp", s, 0), ("zcp", s, 1)] + ([("zc", s, i - 1)] if i > 0 else []) + ([("zc", s, i + 1)] if i < SL // 128 - 1 else [])
                for k in range(5):
                    DMA(xk[b][k], zc_d[base + k:base + k + 128, :], deps, ["xk%d_%d" % (b, k)])
                TT("dve", accA, xk[b][0], cw_[:, 0, :], ALU.mult, ["xk%d_0" % b, "cw"], ["accA"])
                TT("pool", accB, xk[b][3], cw_[:, 3, :], ALU.mult, ["xk%d_3" % b, "cw"], ["accB"])
                for k in (1, 2):
                    TT("dve", tmpc, xk[b][k], cw_[:, k, :], ALU.mult, ["xk%d_%d" % (b, k), "cw"], ["tmpc"])
                    TT("dve", accA, accA, tmpc, ALU.add, ["accA", "tmpc"], ["accA"])
                TT("pool", xk[b][4], xk[b][4], cw_[:, 4, :], ALU.mult, ["xk%d_4" % b, "cw"], ["xk%d_4" % b])
                TT("pool", accB, accB, xk[b][4], ALU.add, ["accB", "xk%d_4" % b], ["accB"])
                TT("dve", accA, accA, accB, ALU.add, ["accA", "accB"], ["accA"])
                Q, kq_ = qkv[b], "qkv%d" % b
                ACTF(Q, accA, AF.Silu, ["accA"], [kq_])
                TT("pool", sqc, Q[:, 0:768], Q[:, 0:768], ALU.mult, [kq_], ["sqc"])
                RED(sc[:, 0:12], sqc.rearrange("p (h d) -> p h d", h=12), ["sqc"], ["sc0"])
                ACTF(sc[:, 12:24], sc[:, 0:12], AF.Sqrt, ["sc0"], ["sc1"], bias=EPS)
                RECIP(sc[:, 24:36], sc[:, 12:24], ["sc1"], ["sc2"])
                TS("dve", sc[:, 24:30], sc[:, 24:30], 0.125, None, ALU.mult, None, ["sc2"], ["sc2"])
                TT("dve", Q[:, 0:768].rearrange("p (h d) -> p h d", h=12), Q[:, 0:768].rearrange("p (h d) -> p h d", h=12),
                   sc[:, 24:36].unsqueeze(2).to_broadcast([128, 12, 64]), ALU.mult, [kq_, "sc2"], [kq_])
                DMA(cq_d[t * 128:(t + 1) * 128, :], Q, [kq_], [("cq", t)])
                ZB, kzb, BG, kbg = zb_[b], "zb%d" % b, bgs[b], "bg%d" % b
                DMA(ZB, z_d[t * 128:(t + 1) * 128, 2944:2968], [("z", t)], [kzb])
                ACTF(BG[:, 0:12], ZB[:, 0:12], AF.Sigmoid, [kzb], [kbg])
                TT("dve", ZB[:, 12:24], ZB[:, 12:24], dtb, ALU.add, [kzb, "dtb"], [kzb])
                ACTF(ZB[:, 12:24], ZB[:, 12:24], AF.Exp, [kzb], [kzb])
                ACTF(ZB[:, 12:24], ZB[:, 12:24], AF.Ln, [kzb], [kzb], bias=1.0)
                TT("dve", BG[:, 12:24], ZB[:, 12:24], nea, ALU.mult, [kzb, "nea"], [kbg])
                DMA(cbg_d[t * 128:(t + 1) * 128, :], BG, [kbg], [("cbg", t)])
            if stop_after == "c0":
                break

            stage_reset()
            cc = a32(4, 128)
            DMA(cc, cconst_d.rearrange("p (a b) -> p a b", a=4), [], ["cc"])
            ones3 = a32(6, 128)
            MEMSET("pool", ones3, 1.0, ["ones3"])
            nwC = a32(384)
            for hh in range(6):
                DMA(nwC[:, hh * 64:(hh + 1) * 64], c_norm_w[l:l + 1, :].partition_broadcast(128), [], ["nwC"])
            CQs = [a32(1152) for _ in range(2)]
            BGs = [a32(24) for _ in range(2)]
            GTs = [a32(384) for _ in range(2)]
            OFs = [a32(384) for _ in range(2)]
            gcs = a32(48)
            gb_all = a32(6, 128)
            xa_, xb_ = a32(6, 128), a32(6, 128)
            DdS, DdT = a32(6, 128), a32(6, 128)
            Xs = [a32(6, 128) for _ in range(2)]
            Ys = [a32(6, 128) for _ in range(2)]
            sol = a32(6, 128)
            qT_, kT_, wT_, qgT_ = a32(3, 128), a32(3, 128), a32(3, 128), a32(3, 128)
            qkTm = a32(6, 128)
            wbuf, qg, kg, vn = a32(384), a32(384), a32(384), a32(384)
            Sst, tmpS, decS = a32(3, 64), a32(3, 64), a32(3)
            osum, osq, on_, sl = a32(384), a32(384), a32(384), a32(384)
            sd = a32(18)
            mt16 = [a16(3, 128) for _ in range(2)]
            mixT_v = mixT_d.rearrange("(b p) t -> p b t", p=128)
            b768 = lambda i0: ps[i0][:, 0:512].rearrange("p (a b) -> p a b", a=4)
            b256 = lambda i0: ps[i0][:, 0:256].rearrange("p (a b) -> p a b", a=2)

            def ev768(eng_a, eng_b, dst, i0, keys_w):
                CP(eng_a, dst[:, 0:4, :], b768(i0), ["ps%d" % i0], keys_w)
                CP(eng_b, dst[:, 4:6, :], b256(i0 + 1), ["ps%d" % (i0 + 1)], keys_w)

            def hslice(h):
                return h // 2, slice((h % 2) * 64, (h % 2) * 64 + 64)

            def pcol(i0, h):
                return ps[i0 + h // 4][:, (h % 4) * 128:(h % 4) * 128 + 128], "ps%d" % (i0 + h // 4)

            ti = 0
            for dr in range(2):
                Ud = cc[:, dr, :]
                m_strict_ij = cc[:, 3 - dr, :]
                m_incl_ji = cc[:, dr, :]
                for (t, first, coupled) in slot_order(dr):
                    b = ti % 2
                    ti += 1
                    if first:
                        if coupled:
                            TS("dve", Sst, Sst, cfl[:, 1:2], None, ALU.mult, None, ["Sst", "cfl"], ["Sst"])
                        else:
                            MEMSET("pool", Sst, 0.0, ["Sst"])
                    CQ, kcq, BG, kbg = CQs[b], "cq%d" % b, BGs[b], "cbg%d" % b
                    DMA(CQ, cq_d[t * 128:(t + 1) * 128, :], [("cq", t)], [kcq])
                    DMA(BG, cbg_d[t * 128:(t + 1) * 128, :], [("cbg", t)], [kbg])
                    q, k, v = CQ[:, 0:384], CQ[:, 384:768], CQ[:, 768:1152]
                    beta, g = BG[:, dr * 6:dr * 6 + 6], BG[:, 12 + dr * 6:18 + dr * 6]
                    MM(ps[6][:, 0:6], Ud, g, ["cc", kbg], ["ps6"])
                    MM(ps[6][:, 8:14], ones3[:, 0, :], g, ["ones3", kbg], ["ps6"])
                    CP("dve", gcs[:, 0:6], ps[6][:, 0:6], ["ps6"], ["gcs"])
                    CP("dve", gcs[:, 6:12], ps[6][:, 8:14], ["ps6"], ["gcs"])
                    ACTF(gcs[:, 12:18], gcs[:, 0:6], AF.Exp, ["gcs"], ["gcs"])
                    TT("dve", gcs[:, 18:24], gcs[:, 6:12], gcs[:, 0:6], ALU.subtract, ["gcs"], ["gcs"])
                    ACTF(gcs[:, 18:24], gcs[:, 18:24], AF.Exp, ["gcs"], ["gcs"])
                    ACTF(gcs[:, 24:30], gcs[:, 6:12], AF.Exp, ["gcs"], ["gcs"])
                    TS("dve", gcs[:, 30:36], beta, -1.0, None, ALU.mult, None, [kbg], ["gcs"])
                    TT("dve", gcs[:, 36:42], beta, gcs[:, 12:18], ALU.mult, [kbg, "gcs"], ["gcs"])
                    TT("pool", gb_all, ones3, g.unsqueeze(2).to_broadcast([128, 6, 128]), ALU.mult, ["ones3", kbg], ["gb"])
                    for h in range(6):
                        o_, ko = pcol(0, h)
                        MM(o_, gb_all[:, h, :], Ud, ["gb", "cc"], [ko])
                    for h in range(6):
                        o_, ko = pcol(0, h)
                        TS("dve", xa_[:, h, :], o_, gcs[:, h:h + 1], 0.0, ALU.subtract, ALU.max, [ko, "gcs"], ["xa"])
                        TS("dve", xb_[:, h, :], o_, gcs[:, h:h + 1], 0.0, ALU.subtract, ALU.min, [ko, "gcs"], ["xb"])
                    ACTF(xa_, xa_, AF.Exp, ["xa"], ["xa"], scale=-1.0)
                    ACTF(xb_, xb_, AF.Exp, ["xb"], ["xb"])
                    TT("pool", DdS, xa_, m_strict_ij.unsqueeze(1).to_broadcast([128, 6, 128]), ALU.mult, ["xa", "cc"], ["DdS"])
                    TT("pool", DdT, xb_, m_incl_ji.unsqueeze(1).to_broadcast([128, 6, 128]), ALU.mult, ["xb", "cc"], ["DdT"])
                    for blk in range(3):
                        TR(ps[4][:, blk * 128:(blk + 1) * 128], q[:, blk * 128:(blk + 1) * 128], [kcq], ["ps4"])
                        TR(ps[5][:, blk * 128:(blk + 1) * 128], k[:, blk * 128:(blk + 1) * 128], [kcq], ["ps5"])
                    CP("act", qT_.rearrange("p a b -> p (a b)"), ps[4][:, 0:384], ["ps4"], ["qT"])
                    CP("act", kT_.rearrange("p a b -> p (a b)"), ps[5][:, 0:384], ["ps5"], ["kT"])
                    for h in range(6):
                        blk, hs = hslice(h)
                        o_, ko = pcol(0, h)
                        MM(o_, kT_[hs, blk, :], kT_[hs, blk, :], ["kT"], [ko])
                        o2, ko2 = pcol(2, h)
                        MM(o2, kT_[hs, blk, :], qT_[hs, blk, :], ["kT", "qT"], [ko2])
                    X, Y = Xs[0], Ys[0]
                    for h in range(6):
                        o_, ko = pcol(0, h)
                        STT(X[:, h, :], o_, gcs[:, 30 + h:31 + h], DdS[:, h, :], ALU.mult, ALU.mult, [ko, "gcs", "DdS"], ["X0"])
                    TT("dve", qkTm[:, 0:4, :], b768(2), DdT[:, 0:4, :], ALU.mult, ["ps2", "DdT"], ["qkTm"])
                    TT("dve", qkTm[:, 4:6, :], b256(3), DdT[:, 4:6, :], ALU.mult, ["ps3", "DdT"], ["qkTm"])
                    for h in range(6):
                        o_, ko = pcol(4, h)
                        TR(o_, X[:, h, :], ["X0"], [ko])
                    ev768("act", "pool" if False else "dve", Y, 4, ["Y0"])
                    s3 = sol.rearrange("p h (a d) -> p h a d", a=2)
                    TT("pool", s3[:, :, 0, :], v.rearrange("p (h d) -> p h d", h=6), beta.unsqueeze(2).to_broadcast([128, 6, 64]), ALU.mult, [kcq, kbg], ["sol"])
                    TT("pool", s3[:, :, 1, :], k.rearrange("p (h d) -> p h d", h=6), gcs[:, 36:42].unsqueeze(2).to_broadcast([128, 6, 64]), ALU.mult, [kcq, "gcs"], ["sol"])
                    for lvl in range(7):
                        kx_, ky_ = "X%d" % (lvl % 2), "Y%d" % (lvl % 2)
                        X, Y = Xs[lvl % 2], Ys[lvl % 2]
                        Xn, Yn = Xs[(lvl + 1) % 2], Ys[(lvl + 1) % 2]
                        for h in range(6):
                            o_, ko = pcol(0, h)
                            MM(o_, Y[:, h, :], sol[:, h, :], [ky_, "sol"], [ko])
                        if lvl < 6:
                            for h in range(6):
                                o2, ko2 = pcol(2, h)
                                MM(o2, Y[:, h, :], X[:, h, :], [kx_, ky_], [ko2])
                                o4, ko4 = pcol(4, h)
                                MM(o4, X[:, h, :], Y[:, h, :], [kx_, ky_], [ko4])
                        TT("dve", sol[:, 0:4, :], sol[:, 0:4, :], b768(0), ALU.add, ["sol", "ps0"], ["sol"])
                        TT("dve", sol[:, 4:6, :], sol[:, 4:6, :], b256(1), ALU.add, ["sol", "ps1"], ["sol"])
                        if lvl < 6:
                            ev768("act", "act", Xn, 2, ["X%d" % ((lvl + 1) % 2)])
                            ev768("dve", "pool" if False else "dve", Yn, 4, ["Y%d" % ((lvl + 1) % 2)])
                    CP("act", wbuf.rearrange("p (h d) -> p h d", h=6), s3[:, :, 1, :], ["sol"], ["wbuf"])
                    TT("pool", qg.rearrange("p (h d) -> p h d", h=6), q.rearrange("p (h d) -> p h d", h=6), gcs[:, 12:18].unsqueeze(2).to_broadcast([128, 6, 64]), ALU.mult, [kcq, "gcs"], ["qg"])
                    TT("pool", kg.rearrange("p (h d) -> p h d", h=6), k.rearrange("p (h d) -> p h d", h=6), gcs[:, 18:24].unsqueeze(2).to_broadcast([128, 6, 64]), ALU.mult, [kcq, "gcs"], ["kg"])
                    for blk in range(3):
                        TR(ps[6][:, blk * 128:(blk + 1) * 128], wbuf[:, blk * 128:(blk + 1) * 128], ["wbuf"], ["ps6"])
                        TR(ps[7][:, blk * 128:(blk + 1) * 128], qg[:, blk * 128:(blk + 1) * 128], ["qg"], ["ps7"])
                    CP("act", wT_.rearrange("p a b -> p (a b)"), ps[6][:, 0:384], ["ps6"], ["wT"])
                    CP("dve", qgT_.rearrange("p a b -> p (a b)"), ps[7][:, 0:384], ["ps7"], ["qgT"])
                    for h in range(6):
                        blk, hs = hslice(h)
                        MM(ps[6][:, h * 64:(h + 1) * 64], wT_[hs, blk, :], Sst[hs, blk, :], ["wT", "Sst"], ["ps6"])
                    TT("dve", vn.rearrange("p (h d) -> p h d", h=6), s3[:, :, 0, :], ps[6][:, 0:384].rearrange("p (h d) -> p h d", h=6), ALU.subtract, ["sol", "ps6"], ["vn"])
                    for h in range(6):
                        blk, hs = hslice(h)
                        MM(ps[7][:, h * 64:(h + 1) * 64], qgT_[hs, blk, :], Sst[hs, blk, :], ["qgT", "Sst"], ["ps7"], start=True, stop=False)
                        MM(ps[7][:, h * 64:(h + 1) * 64], qkTm[:, h, :], vn[:, h * 64:(h + 1) * 64], ["qkTm", "vn"], ["ps7"], start=False, stop=True)
                    for blk in range(3):
                        MM(ps[0][:, blk * 128:(blk + 1) * 128], kg[:, blk * 128:(blk + 1) * 128], vn[:, blk * 128:(blk + 1) * 128], ["kg", "vn"], ["ps0"])
                    g3 = gcs[:, 24:30].rearrange("p (a b) -> p a b", b=2)
                    for half in range(2):
                        hs = slice(half * 64, half * 64 + 64)
                        CP("pool", decS[hs, :], g3[hs, :, half], ["gcs"], ["decS"])
                    TT("dve", tmpS, Sst, decS.unsqueeze(2).to_broadcast([128, 3, 64]), ALU.mult, ["Sst", "decS"], ["tmpS"])
                    p0 = ps[0][:, 0:384].rearrange("p (a b) -> p a b", a=3)
                    for half in range(2):
                        hs = slice(half * 64, half * 64 + 64)
                        TT("dve", Sst[hs], tmpS[hs], p0[hs, :, half * 64:(half + 1) * 64], ALU.add, ["tmpS", "ps0", "ps7", "ps6"], ["Sst"])
                    OF, kof = OFs[b], "cof%d" % b
                    if dr == 0:
                        CP("act", OF, ps[7][:, 0:384], ["ps7"], [kof])
                        DMA(ocf_d[t * 128:(t + 1) * 128, :], OF, [kof], [("ocf", t)])
                    else:
                        GT, kgt = GTs[b], "cgt%d" % b
                        DMA(GT, z_d[t * 128:(t + 1) * 128, 2968:3352], [("z", t)], [kgt])
                        DMA(OF, ocf_d[t * 128:(t + 1) * 128, :], [("ocf", t)], [kof])
                        TT("dve", osum, ps[7][:, 0:384], OF, ALU.add, ["ps7", kof], ["osum"])
                        TT("pool", osq, osum, osum, ALU.mult, ["osum"], ["osq"])
                        RED(sd[:, 0:6], osq.rearrange("p (h d) -> p h d", h=6), ["osq"], ["sd0"])
                        ACTF(sd[:, 6:12], sd[:, 0:6], AF.Sqrt, ["sd0"], ["sd1"], scale=1.0 / 64, bias=EPS)
                        RECIP(sd[:, 12:18], sd[:, 6:12], ["sd1"], ["sd2"])
                        TT("dve", on_.rearrange("p (h d) -> p h d", h=6), osum.rearrange("p (h d) -> p h d", h=6), sd[:, 12:18].unsqueeze(2).to_broadcast([128, 6, 64]), ALU.mult, ["osum", "sd2"], ["on"])
                        TT("pool", on_, on_, nwC, ALU.mult, ["on", "nwC"], ["on"])
                        ACTF(sl, GT, AF.Silu, [kgt], ["sl"])
                        TT("dve", on_, on_, sl, ALU.mult, ["on", "sl"], ["on"])
                        M16, km = mt16[b], "cmt16%d" % b
                        for blk in range(3):
                            TR(ps[1 + blk][:, 0:128], on_[:, blk * 128:(blk + 1) * 128], ["on"], ["ps%d" % (1 + blk)])
                            CP("act" if blk % 2 == 0 else "dve", M16[:, blk, :], ps[1 + blk][:, 0:128], ["ps%d" % (1 + blk)], [km])
                        DMA(mixT_v[:, 4:7, t * 128:(t + 1) * 128], M16, [km], [("mxc", t)])
            if stop_after == "s4":
                break

            assert dl <= 1
            stage_reset()
            dc = a32(8, 128)
            DMA(dc, dconst_d.rearrange("p (a b) -> p a b", a=8), [], ["dc"])
            ones2 = a32(2)
            MEMSET("pool", ones2, 1.0, ["ones2"])
            nwD = a32(384)
            for hh in range(6):
                DMA(nwD[:, hh * 64:(hh + 1) * 64], d_norm_w[l:l + 1, :].partition_broadcast(128), [], ["nwD"])
            if dl > 0:
                lbv, oml, lb0 = a32(768), a32(768), a32(768)
                dlb2 = d_lb.rearrange("l a c -> l (a c)")
                DMA(lbv, dlb2[dl:dl + 1, :].partition_broadcast(128), [], ["lbv"])
                DMA(lb0, dlb2[0:1, :].partition_broadcast(128), [], ["lb0"])
                TT("dve", lbv, lbv, lb0, ALU.subtract, ["lbv", "lb0"], ["lbv"])
                ACTF(lbv, lbv, AF.Sigmoid, ["lbv"], ["lbv"])
                TS("dve", oml, lbv, -1.0, 1.0, ALU.mult, ALU.add, ["lbv"], ["oml"])
            Zs = [a32(1920) for _ in range(2)]
            OFs = [a32(384) for _ in range(2)]
            sg, logf, kk = a32(384), a32(384), a32(384)
            E1, E1i, Em, El = a32(384), a32(384), a32(384), a32(384)
            dec = a32(3, 2)
            qt, kt_, qs, ks = a32(384), a32(384), a32(384), a32(384)
            qtT, ktT, qsT = a32(3, 128), a32(3, 128), a32(3, 128)
            attms = [a32(6, 128) for _ in range(2)]
            for dd in range(2):
                MEMSET("pool", attms[dd], 0.0, ["attm"])
            Sst, tmpS = a32(3, 64), a32(3, 64)
            osum, osq, on_ = a32(384), a32(384), a32(384)
            sl = a32(384)
            sd = a32(18)
            mt16 = [a16(3, 128) for _ in range(2)]
            mixT_v = mixT_d.rearrange("(b p) t -> p b t", p=128)
            ti = 0
            for dr in range(2):
                for (t, first, coupled) in slot_order(dr):
                    b = ti % 2
                    ti += 1
                    Z, kz = Zs[b], "dz%d" % b
                    if first:
                        if coupled:
                            TS("dve", Sst, Sst, cfl[:, 1:2], None, ALU.mult, None, ["Sst", "cfl"], ["Sst"])
                        else:
                            MEMSET("pool", Sst, 0.0, ["Sst"])
                    DMA(Z, z_d[t * 128:(t + 1) * 128, 3352:5272], [("z", t)], [kz])
                    q, vi, fl, gate = Z[:, 0:384], Z[:, 384:768], Z[:, 768 + dr * 384:1152 + dr * 384], Z[:, 1536:1920]
                    ACTF(sg, fl, AF.Sigmoid, [kz], ["sg"])
                    if dl > 0:
                        TT("dve", sg, sg, oml[:, dr * 384:(dr + 1) * 384], ALU.mult, ["sg", "oml"], ["sg"])
                        TT("dve", sg, sg, lbv[:, dr * 384:(dr + 1) * 384], ALU.add, ["sg", "lbv"], ["sg"])
                    ACTF(logf, sg, AF.Ln, ["sg"], ["logf"])
                    TS("pool", kk, sg, -1.0, 1.0, ALU.mult, ALU.add, ["sg"], ["kk"])
                    for m in range(3):
                        MM(ps[m][:, 0:384], dc[:, 3 * dr + m, :], logf, ["dc", "logf"], ["ps%d" % m])
                    for blk in range(3):
                        MM(ps[3][:, blk * 2:blk * 2 + 2], logf[:, blk * 128:(blk + 1) * 128], ones2, ["logf", "ones2"], ["ps3"])
                    ACTF(E1, ps[0][:, 0:384], AF.Exp, ["ps0"], ["E1"])
                    ACTF(E1i, ps[0][:, 0:384], AF.Exp, ["ps0"], ["E1i"], scale=-1.0)
                    ACTF(Em, ps[1][:, 0:384], AF.Exp, ["ps1"], ["Em"])
                    ACTF(El, ps[2][:, 0:384], AF.Exp, ["ps2"], ["El"])
                    ACTF(dec.rearrange("p a b -> p (a b)"), ps[3][:, 0:6], AF.Exp, ["ps3"], ["dec"])
                    TT("dve", qt, q, E1, ALU.mult, [kz, "E1"], ["qt"])
                    TT("pool", kt_, kk, E1i, ALU.mult, ["kk", "E1i"], ["kt"])
                    TT("dve", qs, qt, Em, ALU.mult, ["qt", "Em"], ["qs"])
                    TT("pool", ks, kt_, El, ALU.mult, ["kt", "El"], ["ks"])
                    n = 0
                    for src, ksrc, dstT, kd in ((qt, "qt", qtT, "qtT"), (kt_, "kt", ktT, "ktT"), (qs, "qs", qsT, "qsT")):
                        for blk in range(3):
                            TR(ps[blk][:, 0:128], src[:, blk * 128:(blk + 1) * 128], [ksrc], ["ps%d" % blk])
                            CP("act" if n % 2 == 0 else "dve", dstT[:, blk, :], ps[blk][:, 0:128], ["ps%d" % blk], [kd])
                            n += 1
                    AT = attms[dr]
                    msk = dc[:, 6 + dr, :]
                    for h in range(6):
                        blk, hs = h // 2, slice((h % 2) * 64, (h % 2) * 64 + 64)
                        pb, kpb, c0 = ps[4 + h // 4], "ps%d" % (4 + h // 4), (h % 4) * 128
                        if dr == 0:
                            MM(pb[0:64, c0:c0 + 64], ktT[hs, blk, 0:64], qtT[hs, blk, 0:64], ["ktT", "qtT"], [kpb])
                            MM(pb[:, c0 + 64:c0 + 128], ktT[hs, blk, :], qtT[hs, blk, 64:128], ["ktT", "qtT"], [kpb])
                        else:
                            MM(pb[:, c0:c0 + 64], ktT[hs, blk, :], qtT[hs, blk, 0:64], ["ktT", "qtT"], [kpb])
                            MM(pb[64:128, c0 + 64:c0 + 128], ktT[hs, blk, 64:128], qtT[hs, blk, 64:128], ["ktT", "qtT"], [kpb])
                    for (h0, nh, pb, kpb) in ((0, 4, ps[4], "ps4"), (4, 2, ps[5], "ps5")):
                        pv = pb[:, 0:nh * 128].rearrange("p (a b) -> p a b", a=nh)
                        if dr == 0:
                            regs = ((slice(0, 64), slice(0, 64)), (slice(0, 128), slice(64, 128)))
                        else:
                            regs = ((slice(0, 128), slice(0, 64)), (slice(64, 128), slice(64, 128)))
                        for (pr, cr) in regs:
                            TT("dve", AT[pr, h0:h0 + nh, cr], pv[pr, :, cr], msk[pr, cr].unsqueeze(1).to_broadcast([pr.stop - pr.start, nh, 64]), ALU.mult, [kpb, "dc"], ["attm"])
                    for h in range(6):
                        blk, hs = h // 2, slice((h % 2) * 64, (h % 2) * 64 + 64)
                        MM(ps[6][:, h * 64:(h + 1) * 64], AT[:, h, :], vi[:, h * 64:(h + 1) * 64], ["attm", kz], ["ps6"], start=True, stop=False)
                        MM(ps[6][:, h * 64:(h + 1) * 64], qsT[hs, blk, :], Sst[hs, blk, :], ["qsT", "Sst"], ["ps6"], start=False, stop=True)
                    for blk in range(3):
                        MM(ps[7][:, blk * 128:(blk + 1) * 128], ks[:, blk * 128:(blk + 1) * 128], vi[:, blk * 128:(blk + 1) * 128], ["ks", kz], ["ps7"])
                    TT("dve", tmpS, Sst, dec[:, :, 0:1].to_broadcast([128, 3, 64]), ALU.mult, ["Sst", "dec"], ["tmpS"])
                    p7 = ps[7][:, 0:384].rearrange("p (a b) -> p a b", a=3)
                    for half in range(2):
                        hs = slice(half * 64, half * 64 + 64)
                        TT("dve", Sst[hs], tmpS[hs], p7[hs, :, half * 64:(half + 1) * 64], ALU.add, ["tmpS", "ps7", "ps6"], ["Sst"])
                    OF, kof = OFs[b], "of%d" % b
                    if dr == 0:
                        CP("act", OF, ps[6][:, 0:384], ["ps6"], [kof])
                        DMA(of_d[t * 128:(t + 1) * 128, :], OF, [kof], [("of", t)])
                    else:
                        DMA(OF, of_d[t * 128:(t + 1) * 128, :], [("of", t)], [kof])
                        TT("dve", osum, ps[6][:, 0:384], OF, ALU.add, ["ps6", kof], ["osum"])
                        TT("pool", osq, osum, osum, ALU.mult, ["osum"], ["osq"])
                        RED(sd[:, 0:6], osq.rearrange("p (h d) -> p h d", h=6), ["osq"], ["sd0"])
                        ACTF(sd[:, 6:12], sd[:, 0:6], AF.Sqrt, ["sd0"], ["sd1"], scale=1.0 / 64, bias=EPS)
                        RECIP(sd[:, 12:18], sd[:, 6:12], ["sd1"], ["sd2"])
                        TT("dve", on_.rearrange("p (h d) -> p h d", h=6), osum.rearrange("p (h d) -> p h d", h=6), sd[:, 12:18].unsqueeze(2).to_broadcast([128, 6, 64]), ALU.mult, ["osum", "sd2"], ["on"])
                        TT("pool", on_, on_, nwD, ALU.mult, ["on", "nwD"], ["on"])
                        ACTF(sl, gate, AF.Silu, [kz], ["sl"])
                        TT("dve", on_, on_, sl, ALU.mult, ["on", "sl"], ["on"])
                        M16, km = mt16[b], "mt16%d" % b
                        for blk in range(3):
                            TR(ps[blk][:, 0:128], on_[:, blk * 128:(blk + 1) * 128], ["on"], ["ps%d" % blk])
                            CP("act" if blk % 2 == 0 else "dve", M16[:, blk, :], ps[blk][:, 0:128], ["ps%d" % blk], [km])
                        DMA(mixT_v[:, 7:10, t * 128:(t + 1) * 128], M16, [km], [("mxd", t)])

            if stop_after == "s5":
                break

            stage_reset()
            wo = a16(10, D)
            load_cast(wo, lambda k: w_out[l, k * 128:(k + 1) * 128, :], D, "wo")
            mts = [a16(10, 128) for _ in range(2)]
            xts = [a32(D) for _ in range(2)]
            x1s = [a32(D) for _ in range(2)]
            mixT_v = mixT_d.rearrange("(b p) t -> p b t", p=128)
            for t in range(NT):
                b = t % 2
                MT, X, X1 = mts[b], xts[b], x1s[b]
                kmt, kx, kx1 = "mt%d" % b, "wx%d" % b, "x1%d" % b
                S.dma("sp", lambda e, MT=MT, t=t: e.dma_start(out=MT, in_=mixT_v[:, :, t * 128:(t + 1) * 128]), w=[kmt])
                if not HAVE_C:
                    MEMSET("pool", MT[:, 4:7, :], 0.0, [kmt])
                DMA(X, x_src[t * 128:(t + 1) * 128, :], [], [kx])
                for n in range(2):
                    pb = ps[(2 * t + n) % 4]
                    kp = "ps%d" % ((2 * t + n) % 4)
                    for bb in range(10):
                        S.op("pe", lambda e, pb=pb, MT=MT, bb=bb, n=n: e.matmul(pb[:, 0:512], lhsT=MT[:, bb, :], rhs=wo[:, bb, n * 512:(n + 1) * 512], start=(bb == 0), stop=(bb == 9)), r=[kmt, "wo"], w=[kp])
                    S.op("dve", lambda e, pb=pb, X=X, X1=X1, n=n: e.tensor_tensor(out=X1[:, n * 512:(n + 1) * 512], in0=pb[:, 0:512], in1=X[:, n * 512:(n + 1) * 512], op=ALU.add), r=[kp, kx], w=[kx1])
                S.dma("sp", lambda e, X1=X1, t=t: e.dma_start(out=xa_d[t * 128:(t + 1) * 128, :], in_=X1), r=[kx1], w=[("xa", t)])


          if need_peer:
            stage_reset()
            wq = a32(8, 2048)
            for k in range(8):
                for c0 in range(0, 2048, 1024):
                    DMA(wq[:, k, c0:c0 + 1024], peer_w_query[l, k * 128:(k + 1) * 128, c0:c0 + 1024], [], ["wq"])
            skT = a32(16, 128)
            psc = a32(16, 128)
            DMA(psc, peer_sub_keys[l].rearrange("h p n c -> n (h p) c"), [], ["psc"])
            for hp in range(16):
                TR(ps[hp % 4][:, 0:128], psc[:, hp, :], ["psc"], ["ps%d" % (hp % 4)])
                CP("act" if hp % 2 == 0 else "dve", skT[:, hp, :], ps[hp % 4][:, 0:128], ["ps%d" % (hp % 4)], ["skT"])
            nw2 = a32(D)
            DMA(nw2, norm2_w[l:l + 1, :].partition_broadcast(128), [], ["nw2"])
            ss = mk_ss()
            eqb = a32(16, 256)
            Xs_ = [a32(D) for _ in range(2)]
            Hn = a32(D)
            hT32 = a32(8, 128)
            qTs = a32(16, 128)
            tmp2 = a32(2048)
            psv = a32(16, 16)
            siu = A.alloc(256).bitcast(U32)
            sif = a32(16, 16)
            tiu = A.alloc(128).bitcast(U32)
            pcs = a32(32)
            DMA(pcs, pconst_d[:, :], [], ["pcs"])
            cand, cidx = a32(8, 256), a32(8, 256)
            tv, tvm = a32(8, 16), a32(8, 16)
            eidf = a32(8, 16)
            eidi = A.alloc(128).bitcast(I32)
            pst = a32(16)
            act_, wgt = a32(128), a32(128)
            gb_ = [a32(D) for _ in range(4)]
            NEGBIG = -1.0e30
            for t in range(NT):
                b = t % 2
                X, kx = Xs_[b], "px%d" % b
                DMA(X, xa_d[t * 128:(t + 1) * 128, :], [("xa", t)], [kx])
                rmsnorm_tile(X, kx, Hn, "Hn", nw2, "nw2", ss)
                for k in range(8):
                    TR(ps[k % 2][:, 0:128], Hn[:, k * 128:(k + 1) * 128], ["Hn"], ["ps%d" % (k % 2)])
                    CP("act" if k % 2 == 0 else "dve", hT32[:, k, :], ps[k % 2][:, 0:128], ["ps%d" % (k % 2)], ["hT32"])
                for hp in range(16):
                    pb, kp = ps[4 + hp // 4], "ps%d" % (4 + hp // 4)
                    for k in range(8):
                        MM(pb[:, (hp % 4) * 128:(hp % 4) * 128 + 128], wq[:, k, hp * 128:(hp + 1) * 128], hT32[:, k, :], ["wq", "hT32"], [kp], start=(k == 0), stop=(k == 7))
                for i4 in range(4):
                    CP("act", qTs[:, i4 * 4:(i4 + 1) * 4, :], ps[4 + i4][:, 0:512].rearrange("p (a b) -> p a b", a=4), ["ps%d" % (4 + i4)], ["qTs"])
                for hp in range(16):
                    MM(ps[hp // 4][:, (hp % 4) * 128:(hp % 4) * 128 + 128], qTs[:, hp, :], skT[:, hp, :], ["qTs", "skT"], ["ps%d" % (hp // 4)])
                for i4 in range(4):
                    CP("act" if i4 % 2 == 0 else "dve", psc[:, i4 * 4:(i4 + 1) * 4, :], ps[i4][:, 0:512].rearrange("p (a b) -> p a b", a=4), ["ps%d" % i4], ["psc"])
                t2v = tmp2.rearrange("p (a b) -> p a b", a=16)
                for hp in range(16):
                    S.op("dve", lambda e, hp=hp: e.max(out=psv[:, hp, 0:8], in_=psc[:, hp, :]), r=["psc"], w=["psv"])
                    S.op("dve", lambda e, hp=hp: e.match_replace(out=t2v[:, hp, :], in_to_replace=psv[:, hp, 0:8], in_values=psc[:, hp, :], imm_value=NEGBIG), r=["psc", "psv"], w=["tmp2"])
                    S.op("dve", lambda e, hp=hp: e.max(out=psv[:, hp, 8:16], in_=t2v[:, hp, :]), r=["tmp2"], w=["psv"])
                    S.op("dve", lambda e, hp=hp: e.max_index(out=siu[:, hp * 16:hp * 16 + 8], in_max=psv[:, hp, 0:8], in_values=psc[:, hp, :]), r=["psc", "psv"], w=["siu"])
                    S.op("dve", lambda e, hp=hp: e.max_index(out=siu[:, hp * 16 + 8:hp * 16 + 16], in_max=psv[:, hp, 8:16], in_values=t2v[:, hp, :]), r=["tmp2", "psv"], w=["siu"])
                CP("dve", sif.rearrange("p a b -> p (a b)"), siu, ["siu"], ["sif"])
                psv4 = psv.rearrange("p (h q) k -> p h q k", q=2)
                si4 = sif.rearrange("p (h q) k -> p h q k", q=2)
                c4 = cand.rearrange("p h (a b) -> p h a b", a=16)
                x4 = cidx.rearrange("p h (a b) -> p h a b", a=16)
                TT("dve", c4, psv4[:, :, 0, :].unsqueeze(3).to_broadcast([128, 8, 16, 16]), psv4[:, :, 1, :].unsqueeze(2).to_broadcast([128, 8, 16, 16]), ALU.add, ["psv"], ["cand"])
                TS("dve", si4[:, :, 0, :], si4[:, :, 0, :], 128.0, None, ALU.mult, None, ["sif"], ["sif"])
                TT("dve", x4, si4[:, :, 0, :].unsqueeze(3).to_broadcast([128, 8, 16, 16]), si4[:, :, 1, :].unsqueeze(2).to_broadcast([128, 8, 16, 16]), ALU.add, ["sif"], ["cidx"])
                t2c = tmp2.rearrange("p (a b) -> p a b", a=8)
                for h in range(8):
                    S.op("dve", lambda e, h=h: e.max(out=tv[:, h, 0:8], in_=cand[:, h, :]), r=["cand"], w=["tv"])
                    S.op("dve", lambda e, h=h: e.match_replace(out=t2c[:, h, :], in_to_replace=tv[:, h, 0:8], in_values=cand[:, h, :], imm_value=NEGBIG), r=["cand", "tv"], w=["tmp2"])
                    S.op("dve", lambda e, h=h: e.max(out=tv[:, h, 8:16], in_=t2c[:, h, :]), r=["tmp2"], w=["tv"])
                    S.op("dve", lambda e, h=h: e.max_index(out=tiu[:, h * 16:h * 16 + 8], in_max=tv[:, h, 0:8], in_values=cand[:, h, :]), r=["cand", "tv"], w=["tiu"])
                    S.op("dve", lambda e, h=h: e.max_index(out=tiu[:, h * 16 + 8:h * 16 + 16], in_max=tv[:, h, 8:16], in_values=t2c[:, h, :]), r=["tmp2", "tv"], w=["tiu"])
                for h in range(8):
                    pass
                tif = eqb.rearrange("p a b -> p (a b)")
                TIF = tif[:, 0:128].rearrange("p (h k) -> p h k", h=8)
                AF_ = tif[:, 128:256].rearrange("p (h k) -> p h k", h=8)
                BF_ = tif[:, 256:384].rearrange("p (h k) -> p h k", h=8)
                E0 = tif[:, 384:512].rearrange("p (h k) -> p h k", h=8)
                E1_ = tif[:, 512:640].rearrange("p (h k) -> p h k", h=8)
                W4 = tif[:, 1024:1024 + 2048].rearrange("p (h k a) -> p h k a", h=8, k=16)
                CP("dve", tif[:, 0:128], tiu, ["tiu"], ["eqb"])
                W15 = tif[:, 1024:1024 + 1920].rearrange("p (h k a) -> p h k a", h=8, k=16)
                TT("dve", W15, TIF.unsqueeze(3).to_broadcast([128, 8, 16, 15]), pcs[:, 16:31].unsqueeze(1).unsqueeze(1).to_broadcast([128, 8, 16, 15]), ALU.is_ge, ["eqb", "pcs"], ["eqb"])
                RED(AF_, W15, ["eqb"], ["eqb"])
                STT(BF_.rearrange("p h k -> p (h k)"), AF_.rearrange("p h k -> p (h k)"), -16.0, TIF.rearrange("p h k -> p (h k)"), ALU.mult, ALU.add, ["eqb"], ["eqb"])
                for (SRC, q_, DST) in ((AF_, 0, E0), (BF_, 1, E1_)):
                    TT("dve", W4, SRC.unsqueeze(3).to_broadcast([128, 8, 16, 16]), pcs[:, 0:16].unsqueeze(1).unsqueeze(1).to_broadcast([128, 8, 16, 16]), ALU.is_equal, ["eqb", "pcs"], ["eqb"])
                    TT("dve", W4, W4, si4[:, :, q_, :].unsqueeze(2).to_broadcast([128, 8, 16, 16]), ALU.mult, ["eqb", "sif"], ["eqb"])
                    RED(DST, W4, ["eqb"], ["eqb"])
                TT("dve", eidf, E0, E1_, ALU.add, ["eqb"], ["eidf"])
                if l > 0:
                    TS("dve", eidf, eidf, float(l * 16384), None, ALU.add, None, ["eidf"], ["eidf"])
                CP("dve", eidi, eidf.rearrange("p a b -> p (a b)"), ["eidf"], ["eidi"])
                TT("dve", tvm, tv, tv[:, :, 0:1].to_broadcast([128, 8, 16]), ALU.subtract, ["tv"], ["tvm"])
                ACTF(tvm, tvm, AF.Exp, ["tvm"], ["tvm"])
                RED(pst[:, 0:8], tvm, ["tvm"], ["pst"])
                RECIP(pst[:, 8:16], pst[:, 0:8], ["pst"], ["pst"])
                TT("dve", tvm, tvm, pst[:, 8:16].unsqueeze(2).to_broadcast([128, 8, 16]), ALU.mult, ["tvm", "pst"], ["tvm"])
                junk = eqb.rearrange("p a b -> p (a b)")[:, 0:D]
                for sl_ in range(128):
                    G, kg_ = gb_[sl_ % 4], "gb%d" % (sl_ % 4)
                    S.dma("pool", lambda e, G=G, sl_=sl_, pu=peer_u.rearrange("l e d -> (l e) d"): e.indirect_dma_start(out=G, out_offset=None, in_=pu, in_offset=bass.IndirectOffsetOnAxis(ap=eidi[:, sl_:sl_ + 1], axis=0)),
                          r=["eidi"], w=[kg_])
                    STT(junk, G, 1.0, Hn, ALU.mult, ALU.mult, [kg_, "Hn"], ["eqb", "act"], accum_out=act_[:, sl_:sl_ + 1])
                MEMSET("dve", pst[:, 0:1], 0.0, ["pstg"])
                TS("dve", act_, act_, 1.0, None, ALU.mult, None, ["act", "pstg"], ["act"])
                ACTF(wgt, act_, AF.Gelu, ["act"], ["wgt"])
                TT("dve", wgt, wgt, tvm.rearrange("p a b -> p (a b)"), ALU.mult, ["wgt", "tvm"], ["wgt"])
                for sl_ in range(128):
                    G, kg_ = gb_[sl_ % 4], "gb%d" % (sl_ % 4)
                    S.dma("pool", lambda e, G=G, sl_=sl_, pv=peer_v.rearrange("l e d -> (l e) d"): e.indirect_dma_start(out=G, out_offset=None, in_=pv, in_offset=bass.IndirectOffsetOnAxis(ap=eidi[:, sl_:sl_ + 1], axis=0)),
                          r=["eidi"], w=[kg_])
                    STT(X, G, wgt[:, sl_:sl_ + 1], X, ALU.mult, ALU.add, [kg_, "wgt", kx], [kx])
                DMA(xb_d[t * 128:(t + 1) * 128, :], X, [kx], [("xb", t)])
            x_src = xb_d

        do_final = (mode == "all") or (mode == "peer" and last)
        stage_reset()
        if do_final:
            nwf = a32(D)
            S.dma("sp", lambda e: e.dma_start(out=nwf, in_=final_norm_w[0:1, :].partition_broadcast(128)), w=["nwf"])
            ss = mk_ss()
            xt = [a32(D) for _ in range(2)]
            yt = [a32(D) for _ in range(2)]
            for t in range(NT):
                b = t % 2
                X, Y = xt[b], yt[b]
                kx, ky = "fx%d" % b, "fy%d" % b
                DMA(X, x_src[t * 128:(t + 1) * 128, :], [], [kx])
                rmsnorm_tile(X, kx, Y, ky, nwf, "nwf", ss)
                S.dma("sp", lambda e, Y=Y, t=t: e.dma_start(out=yout[t * 128:(t + 1) * 128, :], in_=Y), r=[ky], w=[("y", t)])
        S.barrier()
        S.emit(block)
    print("instructions:", S.nins, mode, lay)
    _USED[id(nc)] = list(used_inputs)
    return nc


_NC_CACHE = {}
WKEYS = ("norm1_w", "w_in", "a_qnorm_w", "a_knorm_w", "c_conv_w", "c_a_log", "c_dt_bias", "c_norm_w", "d_norm_w",
         "w_out", "norm2_w", "peer_w_query", "peer_sub_keys", "peer_u", "peer_v")


def run_split(NS, SL, DEPTH, inputs, xin_per_core, coupled_per_core):
    ncores = len(xin_per_core)
    consts = [host_consts(NS, SL, cp) for cp in coupled_per_core]
    xs = [np.ascontiguousarray(x, dtype=np.float32) for x in xin_per_core]
    for lay in range(DEPTH):
        for mode in ("mix", "peer"):
            last = (lay == DEPTH - 1)
            key = (NS, SL, mode, lay, last)
            if key not in _NC_CACHE:
                _NC_CACHE[key] = build(NS, SL, 1, mode=mode, lay=lay, last=last)
            nc = _NC_CACHE[key]
            used = _USED[id(nc)]
            shared = {}
            for k in used:
                if k in WKEYS:
                    shared[k] = np.ascontiguousarray(np.asarray(inputs[k], np.float32)[lay:lay + 1])
                elif k == "d_lb":
                    shared[k] = np.ascontiguousarray(np.asarray(inputs[k], np.float32)[0:2])
                elif k == "final_norm_w":
                    shared[k] = np.asarray(inputs[k], np.float32)[None, :]
            in_maps = []
            for c in range(ncores):
                m = dict(shared)
                for k in used:
                    if k in consts[c]:
                        m[k] = consts[c][k]
                m["xin"] = xs[c]
                in_maps.append(m)
            res = run_bass_kernel_spmd(nc, in_maps, core_ids=list(range(ncores)))
            xs = [np.ascontiguousarray(np.asarray(res.results[c]["yout"], np.float32)) for c in range(ncores)]
    return xs


def kernel(**inputs):
    NS, SL, DEPTH = 3, 4096, 2
    xp = np.asarray(inputs["x_prompt"], np.float32)
    xs = np.asarray(inputs["x_sample"], np.float32)
    plan = []
    plan.append([("p", 0, 0), ("p", 0, 1), ("s", 0)])
    plan.append([("p", 1, 0), ("p", 1, 1), ("s", 1)])
    nxt = 2
    for c in range(6):
        n = 3 if c < 2 else 2
        sl = [("s", nxt + i) for i in range(n)]
        nxt += n
        while len(sl) < 3:
            sl.append(("dup", sl[-1][1]))
        plan.append(sl)
    assert nxt == 16

    def slot_x(sl):
        if sl[0] == "p":
            return xp[sl[1], sl[2] * SL:(sl[2] + 1) * SL]
        return xs[sl[1]]

    xin = [np.concatenate([slot_x(sl) for sl in plan[c]], axis=0) for c in range(8)]
    coupled = [plan[c][0][0] == "p" for c in range(8)]
    ys_core = run_split(NS, SL, DEPTH, inputs, xin, coupled)
    yp = np.zeros_like(xp)
    ys = np.zeros_like(xs)
    for c in range(8):
        y = ys_core[c]
        for i, sl in enumerate(plan[c]):
            blk = y[i * SL:(i + 1) * SL]
            if sl[0] == "p":
                yp[sl[1], sl[2] * SL:(sl[2] + 1) * SL] = blk
            elif sl[0] == "s":
                ys[sl[1]] = blk
    return (yp, ys)
```
